# Optimizing a Trainium2 kernel written in Bass

```python
import math
import jax, jax.numpy as jnp
from jax import lax
import numpy as np

D_MODEL = 1024
BATCH = 4
SEQ = 4096
DEPTH = 1

MEM_LEN = 256
RMS_EPS = 1e-6
NEG_INF = -1e30
FORCE = 1e6

POOL_WINDOWS = (2, 4, 8, 16)
POOL_GROUPS = len(POOL_WINDOWS)
POOL_GROUP_DIM = D_MODEL // 8
POOL_WIDTH = POOL_GROUPS * POOL_GROUP_DIM

NSA_HEADS = 16
NSA_KV_HEADS = 4
NSA_GROUP = NSA_HEADS // NSA_KV_HEADS
HEAD_DIM = 64
NSA_WIDTH = NSA_HEADS * HEAD_DIM
KV_WIDTH = NSA_KV_HEADS * HEAD_DIM
CMP_LEN = 32
CMP_STRIDE = 16
SEL_BLOCK = 64
N_SELECT = 16
WINDOW = 512
NSA_Q_BLOCK = 64

X_HEADS = 4
X_HEAD_DIM = 128
X_WIDTH = X_HEADS * X_HEAD_DIM

N_BUCKETS = 32
MAX_DISTANCE = 128

PEER_HEADS = 8
N_KEYS = 128
N_EXPERTS = N_KEYS * N_KEYS
PEER_QDIM = 256
PEER_HALF = PEER_QDIM // 2
PEER_TOPK = 16
PEER_TOKEN_BLOCK = 128

N_BRANCHES = 3
IN_WIDTHS = (POOL_WIDTH, NSA_WIDTH, 6 * KV_WIDTH, 3 * NSA_HEADS, X_WIDTH, N_BRANCHES * D_MODEL)
IN_SPLITS = tuple(int(s) for s in np.cumsum(IN_WIDTHS)[:-1])
D_IN = int(sum(IN_WIDTHS))

kernel_name = "hybrid_pool_nsa_mem_peer_block"


def rms_norm(x, g):
    xf = x.astype(jnp.float32)
    y = xf * lax.rsqrt(jnp.mean(xf * xf, axis=-1, keepdims=True) + RMS_EPS)
    return (y * g.astype(jnp.float32)).astype(x.dtype)


def masked_softmax(s, mask):
    return jax.nn.softmax(jnp.where(mask, s, NEG_INF), axis=-1)


def t5_bucket(dist):
    n = jnp.maximum(dist, 0)
    max_exact = N_BUCKETS // 2
    nf = jnp.maximum(n, 1).astype(jnp.float32)
    large = max_exact + (jnp.log(nf / max_exact) / math.log(MAX_DISTANCE / max_exact)
                         * (N_BUCKETS - max_exact)).astype(jnp.int32)
    large = jnp.minimum(large, N_BUCKETS - 1)
    return jnp.where(n < max_exact, n, large)


def pool_mixer(u, w_grp, scale):
    B, T, _ = u.shape
    c = jnp.cumsum(u.astype(jnp.float32), axis=1)
    t_idx = jnp.arange(T)
    outs = []
    for g, w in enumerate(POOL_WINDOWS):
        sl = slice(g * POOL_GROUP_DIM, (g + 1) * POOL_GROUP_DIM)
        cg = c[..., sl]
        shifted = jnp.pad(cg, ((0, 0), (w, 0), (0, 0)))[:, :T]
        cnt = jnp.minimum(t_idx + 1, w).astype(jnp.float32)[None, :, None]
        outs.append((cg - shifted) / cnt - u[..., sl].astype(jnp.float32))
    pooled = jnp.stack(outs, axis=2)
    mixed = jnp.einsum('btgc,gcd->btgd', pooled, w_grp.astype(jnp.float32))
    return (mixed.reshape(B, T, POOL_WIDTH) * scale.astype(jnp.float32)).astype(u.dtype)


def nsa_attention(q, k_cmp_tok, v_cmp_tok, k_sel, v_sel, k_win, v_win,
                  w_cmp_k, w_cmp_v, pe_k, pe_v, rel_bias):
    B, T = q.shape[:2]
    G, R, dh = NSA_KV_HEADS, NSA_GROUP, HEAD_DIM
    QB = NSA_Q_BLOCK
    n_cmp = (T - CMP_LEN) // CMP_STRIDE + 1
    n_sel = T // SEL_BLOCK
    n_top = min(N_SELECT, n_sel)

    cmp_start = jnp.arange(n_cmp) * CMP_STRIDE
    cmp_end = cmp_start + CMP_LEN - 1
    tok_idx = cmp_start[:, None] + jnp.arange(CMP_LEN)[None, :]

    def compress(tok, pe, w):
        blk = tok[:, tok_idx] + pe[None, None, :, None, :]
        blk = blk.transpose(0, 1, 3, 2, 4).reshape(B, n_cmp, G, CMP_LEN * dh)
        return blk @ w

    kc = compress(k_cmp_tok, pe_k, w_cmp_k)
    vc = compress(v_cmp_tok, pe_v, w_cmp_v)

    sel_start = jnp.arange(n_sel) * SEL_BLOCK
    overlap = ((cmp_start[:, None] < sel_start[None, :] + SEL_BLOCK) &
               (cmp_start[:, None] + CMP_LEN > sel_start[None, :])).astype(jnp.float32)

    ks_blk = k_sel.reshape(B, n_sel, SEL_BLOCK, G, dh).transpose(0, 3, 1, 2, 4)
    vs_blk = v_sel.reshape(B, n_sel, SEL_BLOCK, G, dh).transpose(0, 3, 1, 2, 4)
    kw_pad = jnp.pad(k_win, ((0, 0), (WINDOW, 0), (0, 0), (0, 0)))
    vw_pad = jnp.pad(v_win, ((0, 0), (WINDOW, 0), (0, 0), (0, 0)))

    qg = q.reshape(B, T, G, R, dh) * (dh ** -0.5)
    table_gr = rel_bias.astype(jnp.float32).reshape(N_BUCKETS, G, R)
    bi = jnp.arange(B)[:, None, None, None]
    gi = jnp.arange(G)[None, :, None, None]
    gi5 = jnp.arange(G)[None, :, None, None, None]
    sel_ids = jnp.arange(n_sel)
    win_off = jnp.arange(WINDOW + QB)

    def head_bias(dist):
        return table_gr[t5_bucket(dist)].transpose(2, 3, 0, 1)

    def block(t0):
        qb = lax.dynamic_slice_in_dim(qg, t0, QB, axis=1)
        tq = t0 + jnp.arange(QB)

        dist_c = tq[:, None] - cmp_end[None, :]
        mask_c = dist_c >= 0
        s_c = jnp.einsum('bqgrd,bcgd->bgrqc', qb, kc).astype(jnp.float32) + head_bias(dist_c)
        has = jnp.any(mask_c, axis=-1).astype(jnp.float32)
        p_c = masked_softmax(s_c, mask_c) * has[None, None, None, :, None]
        o_c = jnp.einsum('bgrqc,bcgd->bqgrd', p_c.astype(vc.dtype), vc)

        imp = jnp.einsum('bgrqc,cs->bgqs', p_c, overlap)
        blk_q = tq // SEL_BLOCK
        forced = ((sel_ids[None, :] == 0) | (sel_ids[None, :] == blk_q[:, None]) |
                  (sel_ids[None, :] == blk_q[:, None] - 1))
        causal_blk = sel_ids[None, :] <= blk_q[:, None]
        imp = jnp.where(forced, FORCE, jnp.where(causal_blk, imp, -FORCE))
        _, idx = lax.top_k(imp, n_top)

        k_g = ks_blk[bi, gi, idx]
        v_g = vs_blk[bi, gi, idx]
        pos = idx[..., None] * SEL_BLOCK + jnp.arange(SEL_BLOCK)
        dist_s = tq[None, None, :, None, None] - pos
        bias_s = jnp.moveaxis(table_gr[t5_bucket(dist_s), gi5], -1, 2)
        nk = n_top * SEL_BLOCK
        s_s = jnp.einsum('bqgrd,bgqkld->bgrqkl', qb, k_g).astype(jnp.float32) + bias_s
        s_s = s_s.reshape(B, G, R, QB, nk)
        mask_s = (dist_s >= 0).reshape(B, G, 1, QB, nk)
        p_s = masked_softmax(s_s, mask_s)
        o_s = jnp.einsum('bgrqn,bgqnd->bqgrd', p_s.astype(v_g.dtype), v_g.reshape(B, G, QB, nk, dh))

        k_w = lax.dynamic_slice_in_dim(kw_pad, t0, WINDOW + QB, axis=1)
        v_w = lax.dynamic_slice_in_dim(vw_pad, t0, WINDOW + QB, axis=1)
        pos_w = t0 - WINDOW + win_off
        dist_w = tq[:, None] - pos_w[None, :]
        mask_w = (dist_w >= 0) & (dist_w < WINDOW) & (pos_w[None, :] >= 0)
        s_w = jnp.einsum('bqgrd,bkgd->bgrqk', qb, k_w).astype(jnp.float32) + head_bias(dist_w)
        p_w = masked_softmax(s_w, mask_w)
        o_w = jnp.einsum('bgrqk,bkgd->bqgrd', p_w.astype(v_w.dtype), v_w)
        return (o_c, o_s, o_w)

    o_c, o_s, o_w = lax.map(block, jnp.arange(T // QB) * QB)

    def unblock(o):
        return o.transpose(1, 0, 2, 3, 4, 5).reshape(B, T, NSA_HEADS, dh).astype(q.dtype)

    return unblock(o_c), unblock(o_s), unblock(o_w)


def memory_cross_attention(q_x, mem, g_mem, w_mem_kv):
    B, T, _ = q_x.shape
    M = mem.shape[1]
    kv = rms_norm(mem, g_mem) @ w_mem_kv
    k, v = jnp.split(kv, 2, axis=-1)
    q = q_x.reshape(B, T, X_HEADS, X_HEAD_DIM) * (X_HEAD_DIM ** -0.5)
    k = k.reshape(B, M, X_HEADS, X_HEAD_DIM)
    v = v.reshape(B, M, X_HEADS, X_HEAD_DIM)
    s = jnp.einsum('bthd,bmhd->bhtm', q, k).astype(jnp.float32)
    p = jax.nn.softmax(s, axis=-1).astype(v.dtype)
    return jnp.einsum('bhtm,bmhd->bthd', p, v).reshape(B, T, X_WIDTH)


def peer_ffn(h, w_q, sub_keys, u, v):
    B, T, D = h.shape
    tokens = h.reshape(-1, D)
    blocks = tokens.reshape(-1, PEER_TOKEN_BLOCK, D)
    n = PEER_TOKEN_BLOCK

    def block(xb):
        q = (xb @ w_q).reshape(n, PEER_HEADS, 2, PEER_HALF)
        s = jnp.einsum('nhpc,hpkc->nhpk', q, sub_keys).astype(jnp.float32)
        s_top, i_top = lax.top_k(s, PEER_TOPK)
        cand_s = (s_top[:, :, 0, :, None] + s_top[:, :, 1, None, :]).reshape(n, PEER_HEADS, -1)
        cand_i = (i_top[:, :, 0, :, None] * N_KEYS + i_top[:, :, 1, None, :]).reshape(n, PEER_HEADS, -1)
        best_s, best_pos = lax.top_k(cand_s, PEER_TOPK)
        ids = jnp.take_along_axis(cand_i, best_pos, axis=-1)
        gate = jax.nn.softmax(best_s, axis=-1)
        u_e = u[ids]
        act = jax.nn.gelu(jnp.einsum('nd,nhkd->nhk', xb, u_e).astype(jnp.float32), approximate=False)
        wgt = (gate * act).astype(xb.dtype)
        return jnp.einsum('nhk,nhkd->nd', wgt, v[ids])

    return lax.map(block, blocks).reshape(B, T, D)


def setup_inputs(seed: int = 0) -> dict:
    key = jax.random.key(seed)
    ks = jax.random.split(key, 24)

    def nrm(k, shape, scale):
        return jax.random.normal(k, shape, jnp.float32) * scale

    D = D_MODEL
    return {
        "x": nrm(ks[0], (BATCH, SEQ, D), 1.0),
        "mem": nrm(ks[1], (BATCH, MEM_LEN, D), 1.0),
        "rel_bias": nrm(ks[2], (N_BUCKETS, NSA_HEADS), 0.5),
        "g_mix": 1.0 + nrm(ks[3], (DEPTH, D), 0.05),
        "w_in": nrm(ks[4], (DEPTH, D, D_IN), D ** -0.5),
        "w_pool_grp": nrm(ks[5], (DEPTH, POOL_GROUPS, POOL_GROUP_DIM, POOL_GROUP_DIM), POOL_GROUP_DIM ** -0.5),
        "pool_scale": 1.0 + nrm(ks[6], (DEPTH, POOL_WIDTH), 0.1),
        "w_pool_out": nrm(ks[7], (DEPTH, POOL_WIDTH, D), POOL_WIDTH ** -0.5),
        "w_cmp_k": nrm(ks[8], (DEPTH, CMP_LEN * HEAD_DIM, HEAD_DIM), (CMP_LEN * HEAD_DIM) ** -0.5),
        "w_cmp_v": nrm(ks[9], (DEPTH, CMP_LEN * HEAD_DIM, HEAD_DIM), (CMP_LEN * HEAD_DIM) ** -0.5),
        "pe_k": nrm(ks[10], (DEPTH, CMP_LEN, HEAD_DIM), 0.5),
        "pe_v": nrm(ks[11], (DEPTH, CMP_LEN, HEAD_DIM), 0.5),
        "w_nsa_out": nrm(ks[12], (DEPTH, NSA_WIDTH, D), NSA_WIDTH ** -0.5),
        "g_mem": 1.0 + nrm(ks[13], (DEPTH, D), 0.05),
        "w_mem_kv": nrm(ks[14], (DEPTH, D, 2 * X_WIDTH), D ** -0.5),
        "w_x_out": nrm(ks[15], (DEPTH, X_WIDTH, D), X_WIDTH ** -0.5),
        "w_o": nrm(ks[16], (DEPTH, D, D), D ** -0.5),
        "g_ffn": 1.0 + nrm(ks[17], (DEPTH, D), 0.05),
        "w_peer_q": nrm(ks[18], (DEPTH, D, PEER_HEADS * PEER_QDIM), D ** -0.5),
        "peer_sub_keys": nrm(ks[19], (DEPTH, PEER_HEADS, 2, N_KEYS, PEER_HALF), PEER_HALF ** -0.5),
        "peer_u": nrm(ks[20], (DEPTH, N_EXPERTS, D), D ** -0.5),
        "peer_v": nrm(ks[21], (DEPTH, N_EXPERTS, D), 0.1),
        "g_final": 1.0 + nrm(ks[22], (D,), 0.05),
    }


def reference(x, mem, rel_bias, g_mix, w_in, w_pool_grp, pool_scale, w_pool_out,
              w_cmp_k, w_cmp_v, pe_k, pe_v, w_nsa_out, g_mem, w_mem_kv, w_x_out, w_o,
              g_ffn, w_peer_q, peer_sub_keys, peer_u, peer_v, g_final):
    B, T, D = x.shape
    for l in range(DEPTH):
        hn = rms_norm(x, g_mix[l])
        z = hn @ w_in[l]
        u_pool, q_nsa, kv_nsa, g_nsa, q_x, g_merge = jnp.split(z, IN_SPLITS, axis=-1)

        y_pool = pool_mixer(u_pool, w_pool_grp[l], pool_scale[l]) @ w_pool_out[l]

        kv = kv_nsa.reshape(B, T, 6, NSA_KV_HEADS, HEAD_DIM)
        o_c, o_s, o_w = nsa_attention(q_nsa.reshape(B, T, NSA_HEADS, HEAD_DIM),
                                      kv[:, :, 0], kv[:, :, 1], kv[:, :, 2], kv[:, :, 3],
                                      kv[:, :, 4], kv[:, :, 5],
                                      w_cmp_k[l], w_cmp_v[l], pe_k[l], pe_v[l], rel_bias)
        bg = jax.nn.sigmoid(g_nsa.reshape(B, T, 3, NSA_HEADS))[..., None]
        o_nsa = (bg[:, :, 0] * o_c + bg[:, :, 1] * o_s + bg[:, :, 2] * o_w).reshape(B, T, NSA_WIDTH)
        y_nsa = o_nsa @ w_nsa_out[l]

        y_mem = memory_cross_attention(q_x, mem, g_mem[l], w_mem_kv[l]) @ w_x_out[l]

        gm = jax.nn.sigmoid(g_merge.reshape(B, T, N_BRANCHES, D))
        merged = gm[:, :, 0] * y_pool + gm[:, :, 1] * y_nsa + gm[:, :, 2] * y_mem
        x = x + merged @ w_o[l]

        x = x + peer_ffn(rms_norm(x, g_ffn[l]), w_peer_q[l], peer_sub_keys[l], peer_u[l], peer_v[l])
    return rms_norm(x, g_final)
```

```python
import math
from contextlib import ExitStack
import numpy as np
import ml_dtypes
import concourse.bass as bass
import concourse.mybir as mybir
from concourse.bass_utils import run_bass_kernel_spmd

F32 = mybir.dt.float32
BF16 = mybir.dt.bfloat16
U32 = mybir.dt.uint32
AF = mybir.ActivationFunctionType
ALU = mybir.AluOpType
AX = mybir.AxisListType

import os
NDMA = 24
import os
C2STOP = int(os.environ.get('C2STOP', '99'))
NEG = -30000.0
NS = 16
T = 4096
D = 1024


class Prog:
    CE = ("pe", "dve", "act", "pool")

    def __init__(self, nc, stack):
        self.nc = nc
        self.stack = stack
        self.own_sem = {}
        self.eng = {"pe": nc.tensor, "dve": nc.vector, "act": nc.scalar,
                    "pool": nc.gpsimd, "sp": nc.sync}
        self.stream = {e: [] for e in self.eng}
        self.count = {e: 0 for e in self.CE}
        self.sem = {e: stack.enter_context(nc.semaphore("s_" + e)) for e in self.CE}
        self.dsem = [stack.enter_context(nc.semaphore("s_dma%d" % i)) for i in range(NDMA)]
        self.gsem = [stack.enter_context(nc.semaphore("s_gdma%d" % i)) for i in range(NDMA)]
        self.dma_k = 0
        self.gdma_k = 0
        self.last_w = {}
        self.readers = {}
        self.waited = {e: {} for e in self.eng}
        self.n_ops = 0

    def _wait(self, e, ev):
        sem, val, src = ev
        if src == "pe" and e == "pe":
            return
        key = id(sem)
        if self.waited[e].get(key, 0) >= val:
            return
        self.waited[e][key] = val
        self.stream[e].append(lambda eng, s=sem, v=val: eng.wait_ge(s, v))

    def _deps(self, reads, writes):
        evs = []
        for k in reads:
            if k in self.last_w:
                evs.append(self.last_w[k])
        for k in writes:
            if k in self.last_w:
                evs.append(self.last_w[k])
            evs.extend(self.readers.get(k, []))
        return evs

    def _commit(self, ev, reads, writes):
        for k in reads:
            self.readers.setdefault(k, []).append(ev)
        for k in writes:
            self.last_w[k] = ev
            self.readers[k] = []

    def op(self, e, fn, reads=(), writes=()):
        for ev in self._deps(reads, writes):
            self._wait(e, ev)
        self.count[e] += 1
        sem = self.sem[e]
        self.stream[e].append(lambda eng, f=fn, s=sem: f(eng).then_inc(s, 1))
        ev = (sem, self.count[e], e)
        self._commit(ev, reads, writes)
        self.n_ops += 1
        return ev

    def dma(self, q, fn, reads=(), writes=(), own=None):
        for ev in self._deps(reads, writes):
            self._wait(q, ev)
        if own is not None:
            if own not in self.own_sem:
                self.own_sem[own] = [self.stack.enter_context(self.nc.semaphore("s_own_%s" % own)), 0]
            rec = self.own_sem[own]
            rec[1] += 1
            sem = rec[0]
            self.stream[q].append(lambda eng, f=fn, s=sem: f(eng).then_inc(s, 16))
            ev = (sem, 16 * rec[1], "dma")
            self._commit(ev, reads, writes)
            self.n_ops += 1
            return ev
        if q == "pool":
            k = self.gdma_k
            self.gdma_k += 1
            sem = self.gsem[k % NDMA]
        else:
            k = self.dma_k
            self.dma_k += 1
            sem = self.dsem[k % NDMA]
        rnd = k // NDMA
        if rnd > 0:
            self._wait(q, (sem, 16 * rnd, "dma"))
        self.stream[q].append(lambda eng, f=fn, s=sem: f(eng).then_inc(s, 16))
        ev = (sem, 16 * (rnd + 1), "dma")
        self._commit(ev, reads, writes)
        self.n_ops += 1
        return ev

    def barrier(self):
        evs = [(self.sem[e], self.count[e], e) for e in self.CE if self.count[e] > 0]
        for i in range(NDMA):
            n = (self.dma_k - i + NDMA - 1) // NDMA if self.dma_k > i else 0
            if n > 0:
                evs.append((self.dsem[i], 16 * n, "dma"))
        for i in range(NDMA):
            n = (self.gdma_k - i + NDMA - 1) // NDMA if self.gdma_k > i else 0
            if n > 0:
                evs.append((self.gsem[i], 16 * n, "dma"))
        for rec in self.own_sem.values():
            if rec[1] > 0:
                evs.append((rec[0], 16 * rec[1], "dma"))
        for e in self.eng:
            for sem, val, src in evs:
                key = id(sem)
                if self.waited[e].get(key, 0) >= val:
                    continue
                self.waited[e][key] = val
                self.stream[e].append(lambda eng, s=sem, v=val: eng.wait_ge(s, v))

    def emit(self):
        nc = self.nc
        st = self.stream
        with nc.Block() as block:
            @block.tensor
            def _(eng):
                for f in st["pe"]:
                    f(eng)

            @block.vector
            def _(eng):
                for f in st["dve"]:
                    f(eng)

            @block.scalar
            def _(eng):
                for f in st["act"]:
                    f(eng)

            @block.gpsimd
            def _(eng):
                for f in st["pool"]:
                    f(eng)

            @block.sync
            def _(eng):
                for f in st["sp"]:
                    f(eng)
        self.stream = {e: [] for e in self.eng}


def _t5_bucket(n):
    n = np.maximum(n, 0)
    nf = np.maximum(n, 1).astype(np.float32)
    large = 16 + (np.log(nf / np.float32(16)) / np.float32(math.log(8.0)) * np.float32(16)).astype(np.int32)
    large = np.minimum(large, 31)
    return np.where(n < 16, n, large)


def _oh_rows(dist, valid, minus31):
    L = dist.shape[0]
    oh = np.zeros((33, L), np.float32)
    b = _t5_bucket(dist)
    idx = np.arange(L)
    oh[b[valid], idx[valid]] = 1.0
    if minus31:
        oh[31, idx[valid]] -= 1.0
    oh[32, ~valid] = NEG
    return oh


def _consts(h):
    c = {}
    c["ident"] = np.eye(128, dtype=np.float32)
    c["jflip"] = np.eye(128, dtype=np.float32)[::-1].copy()
    ex = np.zeros((65, 32, 128), np.float32)
    for kt in range(32):
        ex[2 * kt, kt, :64] = 1.0
        ex[2 * kt + 1, kt, 64:] = 1.0
    ex[64] = 1.0
    c["expand"] = ex.reshape(65, 32 * 128)[0:64].astype(ml_dtypes.bfloat16)
    cs = np.arange(256) * 16
    ss = np.arange(64) * 64
    ov = ((cs[:, None] < ss[None, :] + 64) & (cs[:, None] + 32 > ss[None, :])).astype(np.float32)
    ov[255] = 0.0
    c["overlap"] = ov.reshape(2, 128, 64).transpose(1, 0, 2).reshape(128, 128).copy()
    i = np.arange(256)

    def typ(delta, win, minus31):
        if delta is None:
            return _oh_rows(np.zeros(256, np.int64), np.zeros(256, bool), minus31)
        dist = i - 127 + 128 * delta
        valid = dist >= 0
        if win:
            valid &= dist < 512
        return _oh_rows(dist, valid, minus31)
    if h == 0:
        sel = [typ(1, False, True), typ(0, False, True), typ(None, False, True)]
        win = [typ(4, True, False), typ(3, True, False), typ(2, True, False), typ(1, True, False),
               typ(0, True, False), typ(None, True, False)]
    else:
        sel = [typ(2, False, True), typ(1, False, True), typ(0, False, True)]
        win = [typ(None, True, False), typ(4, True, False), typ(3, True, False), typ(2, True, False),
               typ(1, True, False), typ(0, True, False)]
    c["ohw"] = np.concatenate(sel + win, axis=1)
    ii = np.arange(8192)
    dist = ii - 4111 + 128 * h
    c["ohc"] = _oh_rows(dist, dist >= 0, False)
    cm = np.zeros((NS, 128, 64), np.float32)
    ad = np.zeros((NS, 128, 64), np.float32)
    s = np.arange(64)[None, :]
    for j in range(NS):
        t = 128 * (2 * j + h) + np.arange(128)
        bq = (t // 64)[:, None]
        forced = (s == 0) | (s == bq) | (s == bq - 1)
        causal = s <= bq
        cm[j] = (causal & ~forced)
        ad[j] = np.where(forced, 1e6, np.where(causal, 0.0, -1e6))
    c["cm"] = cm.transpose(1, 0, 2).reshape(128, NS * 64).copy()
    c["addt"] = ad.transpose(1, 0, 2).reshape(128, NS * 64).copy()
    selm = np.zeros((48, 48, 64), np.float32)
    for a in range(48):
        selm[a, a, :] = 1.0
    c["gsel"] = selm.reshape(48, 48 * 64).astype(ml_dtypes.bfloat16)
    def amat(first):
        cur = np.zeros((4, 128, 128), np.float32)
        prv = np.zeros((4, 128, 128), np.float32)
        for g, w in enumerate((2, 4, 8, 16)):
            for t in range(128):
                cnt = min(t + 1, w) if first else w
                for d in range(w):
                    tp = t - d
                    if tp >= 0:
                        cur[g, tp, t] += 1.0 / cnt
                    elif not first:
                        prv[g, 128 + tp, t] += 1.0 / cnt
                cur[g, t, t] -= 1.0
        return cur, prv
    gc, gp = amat(False)
    fc, fp = amat(h == 0)
    am = np.stack([gc, gp, fc, fp], 0)
    c["amat"] = am.transpose(2, 0, 1, 3).reshape(128, 16 * 128).copy()
    c["iota"] = np.tile(np.arange(256, dtype=np.float32)[None, :], (128, 1))
    return c


def build_program(dbg=False, with_peer=True, stop_after=None):
    nc = bass.Bass("TRN2", target_bir_lowering=False)

    def din(name, shape, dt=F32):
        return nc.dram_tensor(name, list(shape), dt, kind="ExternalInput")

    def dscr(name, shape, dt, out=False):
        if out:
            return nc.dram_tensor(name, list(shape), dt, kind="ExternalOutput")
        return nc.dram_tensor(name, list(shape), dt)

    x_all = din("x_all", [T, D]).ap()
    x_own = din("x_own", [NS * 128, D]).ap()
    x_prev = din("x_prev", [NS * 128, D]).ap()
    mem = din("mem", [256, D]).ap()
    w_a = din("w_a", [D, 1536]).ap()
    w_b = din("w_b", [D, 5168]).ap()
    rel_bias = din("rel_bias", [32, 16]).ap()
    g_mix = din("g_mix", [1, D]).ap()
    w_pool_grp = din("w_pool_grp", [4, 128, 128]).ap()
    pool_scale = din("pool_scale", [4, 128]).ap()
    w_pool_out = din("w_pool_out", [512, D]).ap()
    w_cmp_k = din("w_cmp_k", [2048, 64]).ap()
    w_cmp_v = din("w_cmp_v", [2048, 64]).ap()
    pe_k = din("pe_k", [32, 64]).ap()
    pe_v = din("pe_v", [32, 64]).ap()
    w_nsa_out = din("w_nsa_out", [D, D]).ap()
    g_mem = din("g_mem", [1, D]).ap()
    w_mem_kv = din("w_mem_kv", [D, D]).ap()
    w_x_out = din("w_x_out", [512, D]).ap()
    w_o = din("w_o", [D, D]).ap()
    g_ffn = din("g_ffn", [1, D]).ap()
    w_peer_q = din("w_peer_q", [D, 2048]).ap()
    sub_keys = din("sub_keys", [16, 128, 128]).ap()
    peer_u = din("peer_u", [16384, D]).ap()
    peer_v = din("peer_v", [16384, D]).ap()
    g_final = din("g_final", [1, D]).ap()
    c_ident = din("c_ident", [128, 128]).ap()
    c_jflip = din("c_jflip", [128, 128]).ap()
    c_expand = din("c_expand", [64, 4096], BF16).ap()
    c_overlap = din("c_overlap", [128, 128]).ap()
    c_ohw = din("c_ohw", [33, 2304]).ap()
    c_ohc = din("c_ohc", [33, 8192]).ap()
    c_cm = din("c_cm", [128, NS * 64]).ap()
    c_addt = din("c_addt", [128, NS * 64]).ap()
    c_gsel = din("c_gsel", [48, 3072], BF16).ap()
    c_amat = din("c_amat", [128, 2048]).ap()
    c_iota = din("c_iota", [128, 256]).ap()

    out = nc.dram_tensor("out", [NS * 128, D], F32, kind="ExternalOutput").ap()

    KT_d = dscr("KT_d", [4, 128, T], BF16)
    V_d = dscr("V_d", [32, 128, 512], BF16)
    QT_d = dscr("QT_d", [8, 128, NS * 128], BF16)
    QX_d = dscr("QX_d", [4, 128, NS * 128], BF16)
    GM_d = dscr("GM_d", [24, 128, NS * 128], BF16)
    YP_d = dscr("YP_d", [NS, 128, 1024], BF16)
    ON_d = dscr("ON_d", [NS, 128, 1024], BF16)
    X1_d = dscr("X1_d", [NS * 128, D], F32, out=dbg)
    fw_d = dscr("fw_d", [16, 2304], BF16)
    fc_d = dscr("fc_d", [16, 8192], BF16)
    UV16_d = dscr("UV16_d", [16384, 2 * D], BF16)

    def dap(h, offset, ap):
        return bass.AP(tensor=h, offset=offset, ap=ap)

    with ExitStack() as st0:
        P = Prog(nc, st0)

        uid = [0]

        def sb(stk, name, shape, dt):
            uid[0] += 1
            return stk.enter_context(nc.sbuf_tensor("%s_%d" % (name, uid[0]), list(shape), dt))

        def ps(stk, name, shape, dt=F32):
            uid[0] += 1
            return stk.enter_context(nc.psum_tensor("%s_%d" % (name, uid[0]), list(shape), dt))

        def ld(dst, src, key, q="sp", reads=()):
            P.dma(q, lambda e: e.dma_start(out=dst, in_=src), reads=list(reads), writes=[key])

        def stg(dst, src, key, q="sp", wkey=None):
            P.dma(q, lambda e: e.dma_start(out=dst, in_=src), reads=[key], writes=[wkey] if wkey else [])

        identf = sb(st0, "identf", [128, 128], F32)
        identb = sb(st0, "identb", [128, 128], BF16)
        jb = sb(st0, "jb", [128, 128], BF16)
        onesb = sb(st0, "onesb", [128, 128], BF16)
        GT = sb(st0, "GT", [48, NS * 128], BF16)
        kcT = sb(st0, "kcT", [128, 2, 256], BF16)
        vc = sb(st0, "vc", [128, 2, 4, 64], BF16)
        ctmp = sb(st0, "ctmp", [128, 128], F32)

        ld(identf[:], c_ident, "identf")
        P.op("dve", lambda e: e.tensor_copy(out=identb[:], in_=identf[:]), ["identf"], ["identb"])
        ld(ctmp[:], c_jflip, "ctmp")
        P.op("dve", lambda e: e.tensor_copy(out=jb[:], in_=ctmp[:]), ["ctmp"], ["jb"])
        P.op("dve", lambda e: e.memset(onesb[:], 1.0), [], ["onesb"])
        P.op("dve", lambda e: e.memset(kcT[:], 0.0), [], ["kcT"])
        P.op("dve", lambda e: e.memset(vc[:], 0.0), [], ["vc"])

        def norm_tile(stk_bufs, src_ap, gtile_key, gtile, dstT, dst_key, i, hn32=None, skip_load=False, xt_ovr=None):
            xt, sq, ss, rs, hn, pT = stk_bufs[i % 2]
            k = str(i % 2)
            xk_ = "xt" + k
            if xt_ovr is not None:
                xt, xk_ = xt_ovr
            if not skip_load:
                ld(xt[:], src_ap, xk_)
            P.op("act", lambda e: e.activation(out=sq[:], in_=xt[:], func=AF.Square, accum_out=ss[:]),
                 [xk_], ["sq" + k, "ss" + k])
            P.op("act", lambda e: e.activation(out=rs[:], in_=ss[:], func=AF.Sqrt, scale=1.0 / D, bias=1e-6),
                 ["ss" + k], ["rs" + k])
            P.op("dve", lambda e: e.reciprocal(out=rs[:], in_=rs[:]), ["rs" + k], ["rs" + k])
            if hn32 is not None:
                P.op("dve", lambda e: e.scalar_tensor_tensor(out=hn32[0], in0=xt[:], scalar=rs[:], in1=gtile[:],
                                                              op0=ALU.mult, op1=ALU.mult),
                     [xk_, "rs" + k, gtile_key], [hn32[1]])
                P.op("act", lambda e: e.copy(out=hn[:], in_=hn32[0]), [hn32[1]], ["hn" + k])
            else:
                P.op("dve", lambda e: e.scalar_tensor_tensor(out=hn[:], in0=xt[:], scalar=rs[:], in1=gtile[:],
                                                              op0=ALU.mult, op1=ALU.mult),
                     [xk_, "rs" + k, gtile_key], ["hn" + k])
            for c in range(8):
                P.op("pe", lambda e, c=c: e.transpose(out=pT[:, c, :], in_=hn[:, c * 128:(c + 1) * 128],
                                                       identity=identb[:]),
                     ["hn" + k, "identb"], ["pT" + k])
            P.op("act", lambda e: e.copy(out=dstT, in_=pT[:]), ["pT" + k], [dst_key])

        def norm_bufs(stk, with_xt=True):
            bufs = []
            for i in range(2):
                bufs.append((sb(stk, "xt%d" % i, [128, D], F32) if with_xt else None, sb(stk, "sq%d" % i, [128, D], F32),
                             sb(stk, "ss%d" % i, [128, 1], F32), sb(stk, "rs%d" % i, [128, 1], F32),
                             sb(stk, "hn%d" % i, [128, D], BF16), ps(stk, "pT%d" % i, [128, 8, 128], BF16)))
            return bufs

        def load_w(wbf, src_cols_ap, ncols, key):
            P.dma("pool", lambda e: e.dma_start(out=wbf[:, :, 0:ncols],
                                                in_=src_cols_ap.rearrange("(c p) n -> p c n", p=128)), [], [key])

        with ExitStack() as s1:
            relb = sb(s1, "relb", [33, 16], F32)
            ohw = sb(s1, "ohw", [33, 2304], F32)
            ohc = sb(s1, "ohc", [33, 8192], F32)
            fsb = sb(s1, "fsb", [16, 8192], BF16)
            fwb = sb(s1, "fwb", [16, 2304], BF16)
            pf = [ps(s1, "pf%d" % i, [16, 512], F32) for i in range(2)]
            P.op("dve", lambda e: e.memset(relb[:], 1.0), [], ["relb"])
            ld(relb[0:32, :], rel_bias, "relb")
            ld(ohw[:], c_ohw, "ohw")
            ld(ohc[:], c_ohc, "ohc")
            n = 0
            for i in range(5):
                w_ = 512 if i < 4 else 256
                pp = pf[n % 2]
                P.op("pe", lambda e, i=i, w_=w_, pp=pp: e.matmul(pp[:, 0:w_], lhsT=relb[:], rhs=ohw[:, i * 512:i * 512 + w_],
                                                                 start=True, stop=True),
                     ["relb", "ohw"], ["pf%d" % (n % 2)])
                P.op("act", lambda e, i=i, w_=w_, pp=pp: e.copy(out=fwb[:, i * 512:i * 512 + w_], in_=pp[:, 0:w_]),
                     ["pf%d" % (n % 2)], ["fwb"])
                n += 1
            for i in range(16):
                pp = pf[n % 2]
                P.op("pe", lambda e, i=i, pp=pp: e.matmul(pp[:], lhsT=relb[:], rhs=ohc[:, i * 512:(i + 1) * 512],
                                                          start=True, stop=True),
                     ["relb", "ohc"], ["pf%d" % (n % 2)])
                P.op("act", lambda e, i=i, pp=pp: e.copy(out=fsb[:, i * 512:(i + 1) * 512], in_=pp[:]),
                     ["pf%d" % (n % 2)], ["fsb"])
                n += 1
            stg(fw_d.ap(), fwb[:], "fwb", wkey="fw_d")
            stg(fc_d.ap(), fsb[:], "fsb", wkey="fc_d")
            P.barrier()
            P.emit()

        if stop_after == 'S':
            return nc
        with ExitStack() as sa:
            hnT = sb(sa, "hnT_all", [128, 8, T], BF16)
            XTc = sb(sa, "XTc", [128, 4, T], BF16)
            gt = sb(sa, "gt_a", [128, D], F32)
            bufs = norm_bufs(sa)
            wbfs = [sb(sa, "wbf%d" % i, [128, 8, 512], BF16) for i in range(2)]
            est = [sb(sa, "est%d" % i, [128, 512], BF16) for i in range(2)]
            pp = [ps(sa, "ppa%d" % i, [128, 512], F32) for i in range(2)]
            ld(gt[:], g_mix.partition_broadcast(128), "gt_a")
            for i in range(32):
                norm_tile(bufs, x_all[i * 128:(i + 1) * 128, :], "gt_a", gt, hnT[:, :, i * 128:(i + 1) * 128],
                          "hnT_all", i)
            n = 0
            load_w(wbfs[0], w_a[:, 0:512], 512, "wbf0")
            for piece in range(3):
                wbf = wbfs[piece % 2]
                wk = "wbf%d" % (piece % 2)
                if piece + 1 < 3:
                    load_w(wbfs[(piece + 1) % 2], w_a[:, (piece + 1) * 512:(piece + 2) * 512], 512, "wbf%d" % ((piece + 1) % 2))
                if piece < 2:
                    for blk in range(4):
                        for tb in range(8):
                            pq = pp[n % 2]
                            for c in range(8):
                                P.op("pe", lambda e, c=c, blk=blk, tb=tb, pq=pq, wbf=wbf: e.matmul(
                                    pq[:], lhsT=wbf[:, c, blk * 128:(blk + 1) * 128],
                                    rhs=hnT[:, c, tb * 512:(tb + 1) * 512], start=(c == 0), stop=(c == 7)),
                                    [wk, "hnT_all"], ["ppa%d" % (n % 2)])
                            if piece == 0:
                                P.op("act", lambda e, blk=blk, tb=tb, pq=pq: e.copy(
                                    out=XTc[:, blk, tb * 512:(tb + 1) * 512], in_=pq[:]),
                                    ["ppa%d" % (n % 2)], ["XTc"])
                            else:
                                es = est[n % 2]
                                P.op("act", lambda e, es=es, pq=pq: e.copy(out=es[:], in_=pq[:]),
                                     ["ppa%d" % (n % 2)], ["est%d" % (n % 2)])
                                stg(KT_d.ap()[blk, :, tb * 512:(tb + 1) * 512], es[:], "est%d" % (n % 2), wkey="KT_d")
                            n += 1
                else:
                    for i in range(32):
                        pq = pp[n % 2]
                        for c in range(8):
                            P.op("pe", lambda e, c=c, i=i, pq=pq, wbf=wbf: e.matmul(
                                pq[:], lhsT=hnT[:, c, i * 128:(i + 1) * 128], rhs=wbf[:, c, :],
                                start=(c == 0), stop=(c == 7)),
                                [wk, "hnT_all"], ["ppa%d" % (n % 2)])
                        es = est[n % 2]
                        P.op("act", lambda e, es=es, pq=pq: e.copy(out=es[:], in_=pq[:]),
                             ["ppa%d" % (n % 2)], ["est%d" % (n % 2)])
                        stg(V_d.ap()[i], es[:], "est%d" % (n % 2), wkey="V_d")
                        n += 1
            wl32 = sb(sa, "wl32", [128, 32, 64], F32)
            WkL = sb(sa, "WkL", [128, 32, 64], BF16)
            WvL = sb(sa, "WvL", [128, 32, 64], BF16)
            pe32 = sb(sa, "pe32", [128, 2, 32], F32)
            peT = sb(sa, "peT", [128, 2, 32], BF16)
            cK = sb(sa, "cK", [128, 1], F32)
            cV = sb(sa, "cV", [1, 64], BF16)
            pcK = ps(sa, "pcK", [128, 8], F32)
            pcV = ps(sa, "pcV", [1, 64], F32)
            pkc = ps(sa, "pkc", [128, 256], F32)
            pvc = ps(sa, "pvc", [128, 256], F32)
            for half in range(2):
                ld(wl32[half * 64:(half + 1) * 64, :, :], w_cmp_k.rearrange("(l d) o -> d l o", d=64), "wl32")
            P.op("dve", lambda e: e.tensor_copy(out=WkL[:], in_=wl32[:]), ["wl32"], ["WkL"])
            for half in range(2):
                ld(wl32[half * 64:(half + 1) * 64, :, :], w_cmp_v.rearrange("(l d) o -> d l o", d=64), "wl32")
            P.op("dve", lambda e: e.tensor_copy(out=WvL[:], in_=wl32[:]), ["wl32"], ["WvL"])
            for half in range(2):
                P.dma("sp", lambda e, half=half: e.dma_start(out=pe32[half * 64:(half + 1) * 64, 0, :],
                                                             in_=pe_k.rearrange("l d -> d l"),
                                                             allow_slow_non_contiguous=True), [], ["pe32"])
                P.dma("sp", lambda e, half=half: e.dma_start(out=pe32[half * 64:(half + 1) * 64, 1, :],
                                                             in_=pe_v.rearrange("l d -> d l"),
                                                             allow_slow_non_contiguous=True), [], ["pe32"])
            P.op("dve", lambda e: e.tensor_copy(out=peT[:], in_=pe32[:]), ["pe32"], ["peT"])
            for half in range(2):
                hs = slice(half * 64, (half + 1) * 64)
                for l in range(32):
                    P.op("pe", lambda e, hs=hs, l=l: e.matmul(pcK[hs, 0:1], lhsT=WkL[hs, l, :], rhs=peT[hs, 0, l:l + 1],
                                                              start=(l == 0), stop=(l == 31)),
                         ["WkL", "peT"], ["pcK"])
            P.op("act", lambda e: e.copy(out=cK[:], in_=pcK[:, 0:1]), ["pcK"], ["cK"])
            for l in range(32):
                P.op("pe", lambda e, l=l: e.matmul(pcV[:], lhsT=peT[0:64, 1, l:l + 1], rhs=WvL[0:64, l, :],
                                                   start=(l == 0), stop=(l == 31)),
                     ["WvL", "peT"], ["pcV"])
            P.op("act", lambda e: e.copy(out=cV[:], in_=pcV[:]), ["pcV"], ["cV"])

            def tokview(blk, hs, l, c0, m):
                v = XTc[hs, blk, :].rearrange("p (c s) -> p c s", s=16)
                return v[:, c0 + l // 16:c0 + l // 16 + m, l % 16]

            for g in range(4):
                hs = slice((g % 2) * 64, (g % 2) * 64 + 64)
                for l in range(32):
                    P.op("pe", lambda e, hs=hs, l=l, g=g: e.matmul(
                        pkc[hs, 0:255], lhsT=WkL[hs, l, :], rhs=tokview(g // 2, hs, l, 0, 255),
                        start=(l == 0), stop=(l == 31)), ["WkL", "XTc"], ["pkc"])
                P.op("act", lambda e, hs=hs, g=g: e.activation(out=kcT[hs, g // 2, 0:255], in_=pkc[hs, 0:255],
                                                               func=AF.Identity, bias=cK[hs, :], scale=1.0),
                     ["pkc", "cK"], ["kcT"])
            for ct in range(2):
                m = 128 if ct == 0 else 127
                for g in range(4):
                    hs = slice((g % 2) * 64, (g % 2) * 64 + 64)
                    for l in range(32):
                        P.op("pe", lambda e, hs=hs, l=l, g=g, ct=ct, m=m: e.matmul(
                            pvc[0:m, g * 64:(g + 1) * 64], lhsT=tokview(2 + g // 2, hs, l, ct * 128, m),
                            rhs=WvL[hs, l, :], start=(l == 0), stop=False), ["WvL", "XTc"], ["pvc"])
                    P.op("pe", lambda e, g=g, m=m: e.matmul(pvc[0:m, g * 64:(g + 1) * 64], lhsT=onesb[0:1, 0:m],
                                                            rhs=cV[:], start=False, stop=True),
                         ["onesb", "cV"], ["pvc"])
                P.op("act", lambda e, ct=ct, m=m: e.copy(out=vc[0:m, ct, :, :], in_=pvc[0:m, :]), ["pvc"], ["vc"])
            P.barrier()
            P.emit()

        if stop_after == 'A':
            return nc
        with ExitStack() as sbk:
            hnT = sb(sbk, "hnT_own", [128, 8, NS * 128], BF16)
            hnTp = sb(sbk, "hnT_prev", [128, 8, NS * 128], BF16)
            gt = sb(sbk, "gt_b", [128, D], F32)
            bufs = norm_bufs(sbk)
            wbfs = [sb(sbk, "wbfb%d" % i, [128, 8, 512], BF16) for i in range(2)]
            est = [sb(sbk, "estb%d" % i, [128, 512], BF16) for i in range(2)]
            pp = [ps(sbk, "ppb%d" % i, [128, 512], F32) for i in range(2)]
            ld(gt[:], g_mix.partition_broadcast(128), "gt_b")
            for i in range(NS):
                norm_tile(bufs, x_own[i * 128:(i + 1) * 128, :], "gt_b", gt, hnT[:, :, i * 128:(i + 1) * 128],
                          "hnT_own", i)
            for i in range(NS):
                norm_tile(bufs, x_prev[i * 128:(i + 1) * 128, :], "gt_b", gt, hnTp[:, :, i * 128:(i + 1) * 128],
                          "hnT_prev", i)
            amat32 = sb(sbk, "amat32", [128, 2048], F32)
            amat = sb(sbk, "amat", [128, 16, 128], BF16)
            wg32 = sb(sbk, "wg32", [128, 4, 128], F32)
            wgb = sb(sbk, "wgb", [128, 4, 128], BF16)
            psc = sb(sbk, "psc", [128, 4], F32)
            wpo32 = sb(sbk, "wpo32", [128, 4, D], F32)
            wpo = sb(sbk, "wpo", [128, 4, D], BF16)
            uo = [sb(sbk, "uo%d" % i, [128, 512], BF16) for i in range(2)]
            up = [sb(sbk, "up%d" % i, [128, 512], BF16) for i in range(2)]
            pooledT = sb(sbk, "pooledT", [128, 4, 128], BF16)
            mixedT = sb(sbk, "mixedT", [128, 4, 128], BF16)
            ypst = [sb(sbk, "ypst%d" % i, [128, D], BF16) for i in range(2)]
            ppl = ps(sbk, "ppl", [128, 4, 128], F32)
            pmx = ps(sbk, "pmx", [128, 4, 128], F32)
            pyp = [ps(sbk, "pyp%d" % i, [128, 4, 128], F32) for i in range(2)]
            ld(amat32[:], c_amat, "amat32")
            P.op("dve", lambda e: e.tensor_copy(out=amat[:].rearrange("p a b -> p (a b)"), in_=amat32[:]),
                 ["amat32"], ["amat"])
            ld(wg32[:], w_pool_grp.rearrange("g c d -> c g d"), "wg32")
            P.op("dve", lambda e: e.tensor_copy(out=wgb[:], in_=wg32[:]), ["wg32"], ["wgb"])
            P.dma("sp", lambda e: e.dma_start(out=psc[:], in_=pool_scale.rearrange("g d -> d g"),
                                               allow_slow_non_contiguous=True), [], ["psc"])
            ld(wpo32[:], w_pool_out.rearrange("(g p) n -> p g n", p=128), "wpo32")
            P.op("pool", lambda e: e.tensor_copy(out=wpo[:], in_=wpo32[:]), ["wpo32"], ["wpo"])
            load_w(wbfs[0], w_b[:, 0:512], 512, "wbfb0")
            load_w(wbfs[1], w_b[:, 512:1024], 512, "wbfb1")
            wbf = wbfs[0]
            n = 0
            for j in range(NS):
                k = j % 2
                for (src, dst, nm) in ((hnT, uo[k], "uo%d" % k), (hnTp, up[k], "up%d" % k)):
                    pq = pp[n % 2]
                    for c in range(8):
                        P.op("pe", lambda e, c=c, j=j, pq=pq, src=src, wbf=wbf: e.matmul(
                            pq[:], lhsT=src[:, c, j * 128:(j + 1) * 128], rhs=wbf[:, c, :],
                            start=(c == 0), stop=(c == 7)), ["wbfb0", "hnT_own", "hnT_prev"], ["ppb%d" % (n % 2)])
                    P.op("act", lambda e, dst=dst, pq=pq: e.copy(out=dst[:], in_=pq[:]), ["ppb%d" % (n % 2)], [nm])
                    n += 1
                kind = 2 if j == 0 else 0
                for g in range(4):
                    P.op("pe", lambda e, g=g, k=k, kind=kind: e.matmul(
                        ppl[:, g, :], lhsT=uo[k][:, g * 128:(g + 1) * 128], rhs=amat[:, kind * 4 + g, :],
                        start=True, stop=False), ["uo%d" % k, "amat"], ["ppl"])
                    P.op("pe", lambda e, g=g, k=k, kind=kind: e.matmul(
                        ppl[:, g, :], lhsT=up[k][:, g * 128:(g + 1) * 128], rhs=amat[:, (kind + 1) * 4 + g, :],
                        start=False, stop=True), ["up%d" % k, "amat"], ["ppl"])
                P.op("dve", lambda e: e.tensor_copy(out=pooledT[:], in_=ppl[:]), ["ppl"], ["pooledT"])
                for g in range(4):
                    P.op("pe", lambda e, g=g: e.matmul(pmx[:, g, :], lhsT=wgb[:, g, :], rhs=pooledT[:, g, :],
                                                       start=True, stop=True), ["wgb", "pooledT"], ["pmx"])
                for g in range(4):
                    P.op("act", lambda e, g=g: e.activation(out=mixedT[:, g, :], in_=pmx[:, g, :], func=AF.Copy,
                                                            scale=psc[:, g:g + 1]), ["pmx", "psc"], ["mixedT"])
                for hf in range(2):
                    for m in range(4):
                        mm = hf * 4 + m
                        for g in range(4):
                            P.op("pe", lambda e, g=g, m=m, mm=mm, hf=hf: e.matmul(
                                pyp[hf][:, m, :], lhsT=wpo[:, g, mm * 128:(mm + 1) * 128], rhs=mixedT[:, g, :],
                                start=(g == 0), stop=(g == 3)), ["wpo", "mixedT"], ["pyp%d" % hf])
                    P.op("dve", lambda e, hf=hf, k=k: e.tensor_copy(
                        out=ypst[k][:, hf * 512:(hf + 1) * 512], in_=pyp[hf][:].rearrange("p a b -> p (a b)")),
                        ["pyp%d" % hf], ["ypst%d" % k])
                stg(YP_d.ap()[j], ypst[k][:], "ypst%d" % k, wkey="YP_d")
            pieces = [(512, "q", 0), (1024, "q", 4), (1584, "x", 0)] + [(2096 + 512 * i, "m", 4 * i) for i in range(6)]
            pieces.append((1536, "g", 0))
            for pi, (col0, kind, b0) in enumerate(pieces):
                wbf = wbfs[(pi + 1) % 2]
                wk = "wbfb%d" % ((pi + 1) % 2)
                if pi + 1 < len(pieces):
                    nc0 = pieces[pi + 1][0]
                    ncols = 48 if pieces[pi + 1][1] == "g" else 512
                    load_w(wbfs[pi % 2], w_b[:, nc0:nc0 + ncols], ncols, "wbfb%d" % (pi % 2))
                if kind == "g":
                    break
                for blk in range(4):
                    for tb in range(NS // 4):
                        pq = pp[n % 2]
                        for c in range(8):
                            P.op("pe", lambda e, c=c, blk=blk, tb=tb, pq=pq, wbf=wbf: e.matmul(
                                pq[:], lhsT=wbf[:, c, blk * 128:(blk + 1) * 128],
                                rhs=hnT[:, c, tb * 512:(tb + 1) * 512], start=(c == 0), stop=(c == 7)),
                                [wk, "hnT_own"], ["ppb%d" % (n % 2)])
                        es = est[n % 2]
                        if kind == "q":
                            P.op("act", lambda e, es=es, pq=pq: e.mul(out=es[:], in_=pq[:], mul=0.125),
                                 ["ppb%d" % (n % 2)], ["estb%d" % (n % 2)])
                            dst = QT_d.ap()[b0 + blk, :, tb * 512:(tb + 1) * 512]
                        elif kind == "x":
                            P.op("act", lambda e, es=es, pq=pq: e.mul(out=es[:], in_=pq[:], mul=128.0 ** -0.5),
                                 ["ppb%d" % (n % 2)], ["estb%d" % (n % 2)])
                            dst = QX_d.ap()[b0 + blk, :, tb * 512:(tb + 1) * 512]
                        else:
                            P.op("act", lambda e, es=es, pq=pq: e.activation(out=es[:], in_=pq[:], func=AF.Sigmoid),
                                 ["ppb%d" % (n % 2)], ["estb%d" % (n % 2)])
                            dst = GM_d.ap()[b0 + blk, :, tb * 512:(tb + 1) * 512]
                        stg(dst, es[:], "estb%d" % (n % 2), wkey="scrB")
                        n += 1
            for tb in range(NS // 4):
                pq = pp[n % 2]
                for c in range(8):
                    P.op("pe", lambda e, c=c, tb=tb, pq=pq, wbf=wbf: e.matmul(
                        pq[0:48, :], lhsT=wbf[:, c, 0:48], rhs=hnT[:, c, tb * 512:(tb + 1) * 512],
                        start=(c == 0), stop=(c == 7)), [wk, "hnT_own"], ["ppb%d" % (n % 2)])
                P.op("act", lambda e, tb=tb, pq=pq: e.activation(out=GT[:, tb * 512:(tb + 1) * 512], in_=pq[0:48, :],
                                                                 func=AF.Sigmoid), ["ppb%d" % (n % 2)], ["GT"])
                n += 1
            P.barrier()
            P.emit()

        if stop_after == 'B':
            return nc
        with ExitStack() as sc:
            KT = sb(sc, "KT", [128, 2, T], BF16)
            KE = sb(sc, "KE", [128, 4, T], BF16)
            QN = [sb(sc, "QN%d" % i, [128, 4, 512], BF16) for i in range(2)]
            b31bc = sb(sc, "b31bc", [128, 16], F32)
            nselW = sb(sc, "nselW", [128, 128], F32)
            Vs = sb(sc, "Vs", [128, 32, 512], BF16)
            BT = sb(sc, "BT", [128, 9, 4, 512], BF16)
            ov32 = sb(sc, "ov32", [128, 128], F32)
            ovl = sb(sc, "ovl", [128, 2, 64], BF16)
            ones1 = sb(sc, "ones1", [128, 64], BF16)
            gsel = sb(sc, "gsel", [48, 48, 64], BF16)
            cm = sb(sc, "cm", [128, NS, 64], F32)
            addt = sb(sc, "addt", [128, NS, 64], F32)
            QT = [sb(sc, "QT%d" % i, [128, 8, 128], BF16) for i in range(2)]
            CB = [sb(sc, "CB%d" % i, [128, 2, 4, 512], BF16) for i in range(2)]
            PT = [sb(sc, "PT%d" % i, [128, 512], BF16) for i in range(4)]
            rz = sb(sc, "rz", [128, 512], F32)
            fac = sb(sc, "fac", [128, 512], F32)
            oaccs = [sb(sc, "oacc%d" % i, [64, 512], F32) for i in range(2)]
            otmp = sb(sc, "otmp", [128, 512], F32)
            impn = sb(sc, "impn", [128, 512], F32)
            impT = sb(sc, "impT", [64, 128], F32)
            impa = sb(sc, "impa", [128, 64], F32)
            impb = sb(sc, "impb", [128, 64], F32)
            v8a = sb(sc, "v8a", [128, 8], F32)
            v8b = sb(sc, "v8b", [128, 8], F32)
            onb = [sb(sc, "onb%d" % i, [64, 4, 512], BF16) for i in range(2)]
            pS = [ps(sc, "pS%d" % i, [128, 512], F32) for i in range(3)]
            pOZ = [ps(sc, "pOZ%d" % i, [128, 512], F32) for i in range(3)]
            pG = ps(sc, "pG", [128, 512], F32)
            pX = ps(sc, "pX", [128, 512], F32)

            for blk in range(2):
                ld(KT[:, blk, :], KT_d.ap()[2 + blk], "KT", reads=["KT_d"])
            for g in range(4):
                ld(KE[0:64, g, :], KT_d.ap()[g // 2, (g % 2) * 64:(g % 2) * 64 + 64, :], "KE", q="sp", reads=["KT_d"])
                ld(KE[64:128, g, :], c_expand, "KE", q="act")
            ld(b31bc[:], rel_bias[31:32, :].partition_broadcast(128), "b31bc")
            P.op("dve", lambda e: e.memset(nselW[:], 0.0), [], ["nselW"])
            for q_ in range(4):
                ld(Vs[:, q_ * 8:(q_ + 1) * 8, :], V_d.ap()[q_ * 8:(q_ + 1) * 8].rearrange("t p n -> p t n"), "Vs",
                   q=("sp", "act")[q_ % 2], reads=["V_d"])
            for ty in range(9):
                for g in range(4):
                    ld(BT[:, ty, g, :].rearrange("p (r q) -> p r q", r=4),
                       dap(fw_d, 4 * g * 2304 + ty * 256, [[1, 128], [2304, 4], [1, 128]]), "BT", q="act", reads=["fw_d"])
            ld(ov32[:], c_overlap, "ov32")
            P.op("dve", lambda e: e.tensor_copy(out=ovl[:].rearrange("p a b -> p (a b)"), in_=ov32[:]),
                 ["ov32"], ["ovl"])
            P.op("dve", lambda e: e.memset(ones1[:], 1.0), [], ["ones1"])
            ld(gsel[:].rearrange("p a b -> p (a b)"), c_gsel, "gsel")
            ld(cm[:].rearrange("p a b -> p (a b)"), c_cm, "cm")
            ld(addt[:].rearrange("p a b -> p (a b)"), c_addt, "addt")
            nS = 0
            nP = 0
            nA = 0
            tstate = [0]

            def t_chunk():
                n_ = tstate[0]
                if n_ >= 256 or not with_peer:
                    return
                tstate[0] += 1
                src, coff = ((peer_u, 0), (peer_v, D))[n_ // 128]
                it = n_ % 128
                i2 = n_ % 2
                P.dma("pool", lambda e: e.dma_start(out=UV16_d.ap()[it * 128:(it + 1) * 128, coff:coff + D],
                                                    in_=src[it * 128:(it + 1) * 128, :]), [], ["tbl16"])

            def c1_loads(j):
                k = j % 2
                ld(QT[k][:], QT_d.ap()[:, :, j * 128:(j + 1) * 128].rearrange("b p t -> p b t"), "QT%d" % k,
                   reads=["scrB"])
                for g in range(4):
                    ld(QN[k][0:64, g, :].rearrange("p (r q) -> p r q", r=4),
                       QT_d.ap()[(g // 2) * 4:(g // 2) * 4 + 4, (g % 2) * 64:(g % 2) * 64 + 64,
                                 j * 128:(j + 1) * 128].rearrange("b p t -> p b t"), "QNq%d_%d" % (k, g), reads=["scrB"])
                for ct in range(2):
                    for g in range(4):
                        off = 256 * j - 2048 * ct + 2048
                        ld(CB[k][:, ct, g, :].rearrange("p (r q) -> p r q", r=4),
                           dap(fc_d, 4 * g * 8192 + off, [[16, 128], [8192, 4], [1, 128]]), "CB%d" % k,
                           reads=["fc_d"])

            def c1_slot(j):
                nonlocal nS, nP, nA
                k = j % 2

                def issue_S(t):
                    nonlocal nS, nP
                    pq = pS[nS % 3]
                    pk = "pS%d" % (nS % 3)
                    nS += 1
                    extra = t["extra"]
                    lhsT_k = t["lk"]
                    rhs_ = t["rhs"]
                    if t["rhs3"]:
                        P.op("pe", lambda e: e.matmul(pq[:].rearrange("p (r q) -> p r q", r=4), lhsT=lhsT_k, rhs=rhs_,
                                                      start=True, stop=(len(extra) == 0)), t["kkeys"] + t["rkeys"], [pk])
                    else:
                        P.op("pe", lambda e: e.matmul(pq[:], lhsT=lhsT_k, rhs=rhs_,
                                                      start=True, stop=(len(extra) == 0)), t["kkeys"] + t["rkeys"], [pk])
                    for xi, (xl, xr, xk) in enumerate(extra):
                        P.op("pe", lambda e, xl=xl, xr=xr, xi=xi: e.matmul(
                            pq[:], lhsT=xl, rhs=xr, start=False, stop=(xi == len(extra) - 1)), xk, [pk])
                    pt = PT[nP % 4]
                    ptk = "PT%d" % (nP % 4)
                    nP += 1
                    P.op("act", lambda e: e.activation(out=pt[:], in_=pq[:], func=AF.Exp), [pk], [ptk])
                    t["pt"], t["ptk"] = pt, ptk

                def consume(t):
                    pt, ptk, acc, acck = t["pt"], t["ptk"], t["acc"], t["acck"]
                    first, last, vl = t["first"], t["last"], t["vl"]
                    P.op("pe", lambda e: e.matmul(acc[0:64, :], lhsT=vl, rhs=pt[:], start=first, stop=last),
                         t["vkeys"] + [ptk], [acck])
                    if t.get("z127"):
                        P.op("pe", lambda e: e.matmul(acc[64:128, :], lhsT=ones1[0:127, :], rhs=pt[0:127, :],
                                                      start=first, stop=last), ["ones1", ptk], [acck])
                    else:
                        P.op("pe", lambda e: e.matmul(acc[64:128, :], lhsT=ones1[:], rhs=pt[:], start=first, stop=last),
                             ["ones1", ptk], [acck])
                    if t["br"] == 0:
                        ct = t["ct"]
                        P.op("pe", lambda e: e.matmul(pG[64:128, :], lhsT=ovl[:, ct, :], rhs=pt[:],
                                                      start=first, stop=last), ["ovl", ptk], ["pI"])

                def finish(br, g, acc, acck):
                    oa = oaccs[g % 2]
                    oak = "oacc%d" % (g % 2)
                    P.op("dve", lambda e: e.tensor_scalar(out=rz[0:64, :], in0=acc[64:128, :], scalar1=1e-30, scalar2=None,
                                                          op0=ALU.max), [acck], ["rz"])
                    P.op("dve", lambda e: e.reciprocal(out=rz[0:64, :], in_=rz[0:64, :]), ["rz"], ["rz"])
                    if br == 0:
                        P.op("dve", lambda e: e.tensor_tensor(out=impn[0:64, :], in0=pG[64:128, :], in1=rz[0:64, :],
                                                              op=ALU.mult), ["pI", "rz"], ["impn"])
                        P.op("dve", lambda e: e.tensor_reduce(out=impT[:], in_=impn[0:64, :].rearrange("p (r q) -> p q r", r=4),
                                                              axis=AX.X, op=ALU.add), ["impn"], ["impT"])
                    for r in range(4):
                        P.op("pe", lambda e, r=r: e.matmul(
                            pG[0:64, r * 128:(r + 1) * 128], lhsT=gsel[:, br * 16 + 4 * g + r, :],
                            rhs=GT[:, j * 128:(j + 1) * 128], start=True, stop=True), ["gsel", "GT"], ["pG"])
                    P.op("dve", lambda e: e.tensor_tensor(out=fac[0:64, :], in0=pG[0:64, :], in1=rz[0:64, :], op=ALU.mult),
                         ["pG", "rz"], ["fac"])
                    if br == 0:
                        P.op("dve", lambda e: e.tensor_tensor(out=oa[0:64, :], in0=acc[0:64, :], in1=fac[0:64, :],
                                                              op=ALU.mult), [acck, "fac"], [oak])
                    else:
                        P.op("dve", lambda e: e.tensor_tensor(out=otmp[0:64, :], in0=acc[0:64, :], in1=fac[0:64, :],
                                                              op=ALU.mult), [acck, "fac"], ["otmp"])
                        if br == 1:
                            P.op("dve", lambda e: e.tensor_tensor(out=onb[k][0:64, g, :], in0=oa[0:64, :], in1=otmp[0:64, :],
                                                                  op=ALU.add), [oak, "otmp"], ["onb%d_%d" % (k, g)])
                        else:
                            P.op("dve", lambda e: e.tensor_tensor(out=oa[0:64, :], in0=oa[0:64, :], in1=otmp[0:64, :],
                                                                  op=ALU.add), [oak, "otmp"], [oak])
                    if br == 1:
                        hb = (g % 2) * 64
                        stg(ON_d.ap()[j][hb:hb + 64, (g // 2) * 512:(g // 2 + 1) * 512], onb[k][0:64, g, :],
                            "onb%d_%d" % (k, g), wkey="ON_d")

                def topk2(g):
                    P.op("pe", lambda e: e.transpose(out=pX[:, 0:64], in_=impT[:], identity=identf[0:64, 0:64]),
                         ["impT", "identf"], ["pX"])
                    P.op("dve", lambda e: e.tensor_tensor(out=impa[:], in0=pX[:, 0:64], in1=cm[:, j, :], op=ALU.mult),
                         ["pX", "cm"], ["impa"])
                    P.op("dve", lambda e: e.tensor_tensor(out=impa[:], in0=impa[:], in1=addt[:, j, :], op=ALU.add),
                         ["impa", "addt"], ["impa"])
                    P.op("dve", lambda e: e.max(out=v8a[:], in_=impa[:]), ["impa"], ["v8a"])
                    P.op("dve", lambda e: e.match_replace(out=impb[:], in_to_replace=v8a[:], in_values=impa[:],
                                                          imm_value=-3e38), ["impa", "v8a"], ["impb"])
                    P.op("dve", lambda e: e.max(out=v8b[:], in_=impb[:]), ["impb"], ["v8b"])
                    P.op("dve", lambda e: e.tensor_scalar(out=nselW[:, 64:128], in0=impa[:], scalar1=v8b[:, 7:8], scalar2=NEG,
                                                          op0=ALU.is_lt, op1=ALU.mult), ["impa", "v8b"], ["nselW"])

                def topk3(g):
                    P.op("pe", lambda e: e.transpose(out=pX[:, 128:256], in_=nselW[:], identity=identf[:]),
                         ["nselW", "identf"], ["pX"])
                    for r in range(4):
                        P.op("act", lambda e, r=r: e.activation(
                            out=QN[k][64:128, g, r * 128:(r + 1) * 128], in_=pX[64:128, 128:256], func=AF.Identity,
                            bias=b31bc[64:128, 4 * g + r:4 * g + r + 1], scale=1.0), ["pX", "b31bc"], ["QNn%d_%d" % (k, g)])

                def branch_tiles(br, g):
                    nonlocal nA
                    hb = (g % 2) * 64
                    hs = slice(hb, hb + 64)
                    pr = g // 2
                    qv = QT[k][hs, pr * 4:(pr + 1) * 4, :]
                    qkey = ["QT%d" % k]
                    acc, acck = pOZ[nA % 3], "pOZ%d" % (nA % 3)
                    nA += 1
                    out_ = []
                    if br == 0:
                        for ct in range(2):
                            out_.append(dict(br=0, g=g, ct=ct, lk=kcT[hs, pr, ct * 128:(ct + 1) * 128], kkeys=["kcT"],
                                             rhs=qv, rhs3=True, rkeys=qkey,
                                             extra=[(jb[:], CB[k][:, ct, g, :], ["jb", "CB%d" % k])],
                                             vl=vc[:, ct, g, :], vkeys=["vc"], z127=(ct == 1),
                                             first=(ct == 0), last=(ct == 1), acc=acc, acck=acck))
                    elif br == 2:
                        kts = [kt for kt in range(2 * j - 4, 2 * j + 2) if kt >= 0]
                        for ki, kt in enumerate(kts):
                            ty = 3 + kt - (2 * j - 4)
                            out_.append(dict(br=2, g=g, lk=KT[hs, pr, kt * 128:(kt + 1) * 128], kkeys=["KT"],
                                             rhs=qv, rhs3=True, rkeys=qkey,
                                             extra=[(jb[:], BT[:, ty, g, :], ["jb", "BT"])],
                                             vl=Vs[:, kt, 256 + g * 64:256 + (g + 1) * 64], vkeys=["Vs"],
                                             first=(ki == 0), last=(ki == len(kts) - 1), acc=acc, acck=acck))
                    else:
                        nk = 2 * j + 2
                        for kt in range(nk):
                            extra = []
                            if kt >= 2 * j - 1:
                                ty = kt - (2 * j - 1)
                                extra.append((jb[:], BT[:, ty, g, :], ["jb", "BT"]))
                            out_.append(dict(br=1, g=g, lk=KE[:, g, kt * 128:(kt + 1) * 128], kkeys=["KE"], extra=extra,
                                             rhs=QN[k][:, g, :], rhs3=False,
                                             rkeys=["QNq%d_%d" % (k, g), "QNn%d_%d" % (k, g)],
                                             vl=Vs[:, kt, g * 64:(g + 1) * 64], vkeys=["Vs"],
                                             first=(kt == 0), last=(kt == nk - 1), acc=acc, acck=acck))
                    return out_

                order = [(0, 0), (2, 0), (0, 1), (2, 1), (1, 0), (0, 2), (2, 2), (1, 1), (0, 3), (2, 3), (1, 2), (1, 3)]
                seq = []
                for (br, g) in order:
                    seq.extend(branch_tiles(br, g))
                for pos_, t in enumerate(seq):
                    t["F"] = (issue_S, consume, finish, topk2, topk3)
                    t["j"] = j
                    t["pos"] = pos_
                    t["nseq"] = len(seq)
                return seq

            c1_loads(0)
            allseq = []
            for j in range(NS):
                allseq.extend(c1_slot(j))
            LA = 2
            pend = []
            for idx in range(len(allseq) + LA):
                if idx < len(allseq):
                    t = allseq[idx]
                    if t["pos"] == 0 and t["j"] + 1 < NS:
                        c1_loads(t["j"] + 1)
                    if t["br"] == 1 and t["first"]:
                        keep = []
                        for p_ in pend:
                            if p_[2] == t["g"] and p_[3] == t["j"]:
                                p_[1](p_[2])
                            else:
                                keep.append(p_)
                        pend = keep
                    t["F"][0](t)
                    if (t["pos"] * 16) // t["nseq"] != ((t["pos"] + 1) * 16) // t["nseq"]:
                        t_chunk()
                keep = []
                for p_ in pend:
                    if p_[0] <= idx:
                        p_[1](p_[2])
                    else:
                        keep.append(p_)
                pend = keep
                if idx >= LA:
                    t = allseq[idx - LA]
                    t["F"][1](t)
                    if t["last"]:
                        t["F"][2](t["br"], t["g"], t["acc"], t["acck"])
                        if t["br"] == 0:
                            pend.append((idx + 2, t["F"][3], t["g"], t["j"]))
                            pend.append((idx + 4, t["F"][4], t["g"], t["j"]))
            for p_ in pend:
                p_[1](p_[2])
            for _ in range(256):
                t_chunk()
            P.barrier()
            P.emit()
        if stop_after == 'C1':
            return nc
        with ExitStack() as sd:
            Wno = sb(sd, "Wno", [128, 8, D], BF16)
            Wxo = sb(sd, "Wxo", [128, 4, D], BF16)
            Wo = sb(sd, "Wo", [128, 8, D], BF16)
            Wm = sb(sd, "Wm", [128, 8, D], BF16)
            gtm = sb(sd, "gt_m", [128, D], F32)
            hnTm = sb(sd, "hnTm", [128, 8, 256], BF16)
            KmT = sb(sd, "KmT", [128, 4, 256], BF16)
            Vm = sb(sd, "Vm", [128, 2, 512], BF16)
            bufs = norm_bufs(sd)
            onT = [sb(sd, "onTc%d" % i, [128, 8, 128], BF16) for i in range(2)]
            qxT = [sb(sd, "qxT%d" % i, [128, 4, 128], BF16) for i in range(2)]
            gm = [sb(sd, "gm%d" % i, [128, 24, 128], BF16) for i in range(2)]
            yp = [sb(sd, "yp%d" % i, [128, 8, 128], BF16) for i in range(2)]
            xo = [sb(sd, "xo%d" % i, [128, D], F32) for i in range(2)]
            acc = sb(sd, "acc", [128, 8, 128], F32)
            tmp = sb(sd, "tmpc", [128, 8, 128], F32)
            tmp2 = sb(sd, "tmp2c", [128, 8, 128], F32)
            mrgT = sb(sd, "mrgT", [128, 8, 128], BF16)
            x1t = [sb(sd, "x1t%d" % i, [128, D], F32) for i in range(2)]
            PTm = [sb(sd, "PTm%d" % i, [128, 512], BF16) for i in range(2)]
            oxT = sb(sd, "oxT", [128, 4, 128], BF16)
            rzm = sb(sd, "rzm", [128, 512], F32)
            pY = [ps(sd, "pY%d" % i, [128, 4, 128], F32) for i in range(2)]
            pS2 = [ps(sd, "pSm%d" % i, [128, 512], F32) for i in range(2)]
            pO2 = ps(sd, "pOm", [128, 512], F32)
            pZ2 = ps(sd, "pZm", [128, 512], F32)

            def ldc(dst, src, key):
                P.dma("pool", lambda e: e.dma_start(out=dst, in_=src), [], [key])
            for half in range(2):
                for a_ in range(2):
                    ldc(Wno[half * 64:(half + 1) * 64, a_ * 4:(a_ + 1) * 4, :],
                        w_nsa_out.rearrange("(a hf r d) n -> hf a d r n", a=2, hf=2, r=4, d=64)[half, a_], "Wno")
            for q_ in range(2):
                ldc(Wm[:, q_ * 4:(q_ + 1) * 4, :], w_mem_kv.rearrange("(c p) n -> p c n", p=128)[:, q_ * 4:(q_ + 1) * 4, :], "Wm")
            for q_ in range(2):
                ldc(Wo[:, q_ * 4:(q_ + 1) * 4, :], w_o.rearrange("(c p) n -> p c n", p=128)[:, q_ * 4:(q_ + 1) * 4, :], "Wo")
            ldc(Wxo[:], w_x_out.rearrange("(c p) n -> p c n", p=128), "Wxo")
            ld(gtm[:], g_mem.partition_broadcast(128), "gt_m")
            for i in range(2):
                norm_tile(bufs, mem[i * 128:(i + 1) * 128, :], "gt_m", gtm, hnTm[:, :, i * 128:(i + 1) * 128], "hnTm", i)
            for hx in range(4):
                for c in range(8):
                    P.op("pe", lambda e, c=c, hx=hx: e.matmul(pS2[hx % 2][:, 0:256], lhsT=Wm[:, c, hx * 128:(hx + 1) * 128],
                                                              rhs=hnTm[:, c, :], start=(c == 0), stop=(c == 7)),
                         ["Wm", "hnTm"], ["pSm%d" % (hx % 2)])
                P.op("act", lambda e, hx=hx: e.copy(out=KmT[:, hx, :], in_=pS2[hx % 2][:, 0:256]),
                     ["pSm%d" % (hx % 2)], ["KmT"])
            for mt in range(2):
                for c in range(8):
                    P.op("pe", lambda e, c=c, mt=mt: e.matmul(pS2[mt][:], lhsT=hnTm[:, c, mt * 128:(mt + 1) * 128],
                                                              rhs=Wm[:, c, 512:1024], start=(c == 0), stop=(c == 7)),
                         ["Wm", "hnTm"], ["pSm%d" % mt])
                P.op("act", lambda e, mt=mt: e.copy(out=Vm[:, mt, :], in_=pS2[mt][:]), ["pSm%d" % mt], ["Vm"])

            if stop_after == 'C2a':
                P.barrier()
                P.emit()
                return nc

            def c2_slot(j):
                k = j % 2
                ld(onT[k][:], ON_d.ap()[j].rearrange("p (a t) -> p a t", a=8), "onTc%d" % k, reads=["ON_d"])
                ld(qxT[k][:], QX_d.ap()[:, :, j * 128:(j + 1) * 128].rearrange("b p t -> p b t"), "qxT%d" % k,
                   reads=["scrB"])
                ld(gm[k][:], GM_d.ap()[:, :, j * 128:(j + 1) * 128].rearrange("b p t -> p b t"), "gm%d" % k,
                   reads=["scrB"])
                ld(yp[k][:], YP_d.ap()[j].rearrange("p (a t) -> p a t", a=8), "yp%d" % k, reads=["YP_d"])
                ld(xo[k][:], x_own[j * 128:(j + 1) * 128, :], "xo%d" % k)
                if C2STOP <= 1:
                    return
                for hf in range(2):
                    for m4 in range(4):
                        m = hf * 4 + m4
                        for hh in range(8):
                            P.op("pe", lambda e, m=m, m4=m4, hf=hf, hh=hh: e.matmul(
                                pY[hf][:, m4, :], lhsT=Wno[:, hh, m * 128:(m + 1) * 128],
                                rhs=onT[k][:, hh, :], start=(hh == 0), stop=(hh == 7)),
                                ["Wno", "onTc%d" % k], ["pY%d" % hf])
                    P.op("dve", lambda e, hf=hf: e.tensor_tensor(out=acc[:, hf * 4:(hf + 1) * 4, :], in0=pY[hf][:],
                                                                 in1=gm[k][:, 8 + hf * 4:8 + (hf + 1) * 4, :], op=ALU.mult),
                         ["pY%d" % hf, "gm%d" % k], ["acc"])
                if C2STOP <= 2:
                    return
                for mt in range(2):
                    for hx in range(4):
                        P.op("pe", lambda e, hx=hx, mt=mt: e.matmul(
                            pS2[mt][:, hx * 128:(hx + 1) * 128], lhsT=KmT[:, hx, mt * 128:(mt + 1) * 128],
                            rhs=qxT[k][:, hx, :], start=True, stop=True), ["KmT", "qxT%d" % k], ["pSm%d" % mt])
                    P.op("act", lambda e, mt=mt: e.activation(out=PTm[mt][:], in_=pS2[mt][:], func=AF.Exp),
                         ["pSm%d" % mt], ["PTm%d" % mt])
                if C2STOP <= 3:
                    return
                for mt in range(2):
                    P.op("pe", lambda e, mt=mt: e.matmul(pZ2[:], lhsT=onesb[:], rhs=PTm[mt][:], start=(mt == 0),
                                                         stop=(mt == 1)), ["onesb", "PTm%d" % mt], ["pZm"])
                for hx in range(4):
                    for mt in range(2):
                        P.op("pe", lambda e, hx=hx, mt=mt: e.matmul(
                            pO2[:, hx * 128:(hx + 1) * 128], lhsT=Vm[:, mt, hx * 128:(hx + 1) * 128],
                            rhs=PTm[mt][:, hx * 128:(hx + 1) * 128], start=(mt == 0), stop=(mt == 1)),
                            ["Vm", "PTm%d" % mt], ["pOm"])
                P.op("dve", lambda e: e.reciprocal(out=rzm[:], in_=pZ2[:]), ["pZm"], ["rzm"])
                P.op("dve", lambda e: e.tensor_tensor(out=oxT[:].rearrange("p a b -> p (a b)"), in0=pO2[:], in1=rzm[:],
                                                      op=ALU.mult), ["pOm", "rzm"], ["oxT"])
                if C2STOP <= 4:
                    return
                for hf in range(2):
                    for m4 in range(4):
                        m = hf * 4 + m4
                        for hx in range(4):
                            P.op("pe", lambda e, hx=hx, m=m, m4=m4, hf=hf: e.matmul(
                                pY[hf][:, m4, :], lhsT=Wxo[:, hx, m * 128:(m + 1) * 128], rhs=oxT[:, hx, :],
                                start=(hx == 0), stop=(hx == 3)), ["Wxo", "oxT"], ["pY%d" % hf])
                    P.op("dve", lambda e, hf=hf: e.tensor_tensor(out=tmp[:, hf * 4:(hf + 1) * 4, :], in0=pY[hf][:],
                                                                 in1=gm[k][:, 16 + hf * 4:16 + (hf + 1) * 4, :], op=ALU.mult),
                         ["pY%d" % hf, "gm%d" % k], ["tmpc"])
                if C2STOP <= 5:
                    return
                P.op("pool", lambda e: e.tensor_tensor(out=tmp2[:], in0=gm[k][:, 0:8, :], in1=yp[k][:], op=ALU.mult),
                     ["gm%d" % k, "yp%d" % k], ["tmp2c"])
                P.op("pool", lambda e: e.tensor_tensor(out=tmp2[:], in0=tmp2[:], in1=tmp[:], op=ALU.add),
                     ["tmp2c", "tmpc"], ["tmp2c"])
                P.op("dve", lambda e: e.tensor_tensor(out=mrgT[:], in0=acc[:], in1=tmp2[:], op=ALU.add),
                     ["acc", "tmp2c"], ["mrgT"])
                if C2STOP <= 6:
                    return
                for nh in range(2):
                    for c in range(8):
                        P.op("pe", lambda e, c=c, nh=nh: e.matmul(
                            pY[nh][:].rearrange("p a b -> p (a b)"), lhsT=mrgT[:, c, :], rhs=Wo[:, c, nh * 512:(nh + 1) * 512],
                            start=(c == 0), stop=(c == 7)), ["mrgT", "Wo"], ["pY%d" % nh])
                    P.op("dve", lambda e, nh=nh: e.tensor_tensor(
                        out=x1t[k][:, nh * 512:(nh + 1) * 512], in0=pY[nh][:].rearrange("p a b -> p (a b)"),
                        in1=xo[k][:, nh * 512:(nh + 1) * 512], op=ALU.add), ["pY%d" % nh, "xo%d" % k], ["x1t%d" % k])
                stg(X1_d.ap()[j * 128:(j + 1) * 128, :], x1t[k][:], "x1t%d" % k, wkey="X1_d")
            for j in range(1 if stop_after == 'C2b' else NS):
                c2_slot(j)
            P.barrier()
            P.emit()

        if stop_after == 'C2':
            return nc
        if with_peer:
          with ExitStack() as se:
            Wq = sb(se, "Wq", [128, 8, 2048], BF16)
            skT = sb(se, "skT", [128, 16, 128], F32)
            gff = sb(se, "gff", [128, D], F32)
            gfn = sb(se, "gfn", [128, D], F32)
            bufs = norm_bufs(se, with_xt=False)
            hn32b = [sb(se, "hn32_%d" % i, [128, D], F32) for i in range(2)]
            x1b = [sb(se, "x1b%d" % i, [128, D], F32) for i in range(3)]
            hn2T = sb(se, "hn2T", [128, 8, 128], BF16)
            qT = sb(se, "qT", [128, 16, 128], F32)
            S = sb(se, "S", [128, 16, 128], F32)
            S2 = sb(se, "S2", [128, 128], F32)
            tv = sb(se, "tv", [128, 16, 16], F32)
            ti = sb(se, "ti", [128, 16, 16], U32)
            tif = sb(se, "tif", [128, 16, 16], F32)
            ti0 = sb(se, "ti0", [128, 8, 16], F32)
            cs_ = sb(se, "cands", [128, 8, 256], F32)
            cid = sb(se, "candid", [128, 8, 256], F32)
            c2 = sb(se, "c2", [128, 256], F32)
            junk = sb(se, "junk", [128, D], F32)
            bs = sb(se, "bs", [128, 8, 16], F32)
            bpos = sb(se, "bpos", [128, 8, 16], U32)
            pu = sb(se, "pu", [128, 2, 8, 16], U32)
            pf = sb(se, "pf", [128, 2, 8, 16], F32)
            idp = sb(se, "idp", [128, 2, 128], F32)
            iota = sb(se, "iota", [128, 256], F32)
            nb = sb(se, "nb", [128, 8], F32)
            zs_ = sb(se, "zs", [128, 8], F32)
            idf = sb(se, "idf", [128, 128], F32)
            ids2 = [sb(se, "ids%d" % i, [128, 128], U32) for i in range(3)]
            eg = sb(se, "eg", [128, 8, 16], F32)
            gateb = [sb(se, "gate%d" % i, [128, 8, 16], F32) for i in range(2)]
            actv = sb(se, "actv", [128, 128], F32)
            NB = 12
            UVb = [sb(se, "UVb%d" % i, [128, 2 * D], BF16) for i in range(NB)]
            dg = [sb(se, "dg%d" % i, [128, 128], BF16) for i in range(4)]
            dgA = [sb(se, "dgA%d" % i, [128, 128], BF16) for i in range(4)]
            gsb = sb(se, "gsb", [128, 8], F32)
            x2 = sb(se, "x2", [128, D], F32)
            ss2 = sb(se, "ss2", [128, 1], F32)
            rs2 = sb(se, "rs2", [128, 1], F32)
            outt = sb(se, "outt", [128, D], F32)
            pQ = [ps(se, "pQ%d" % i, [128, 4, 128], F32) for i in range(2)]
            pS3 = [ps(se, "pS3%d" % i, [128, 4, 128], F32) for i in range(2)]
            pV = [ps(se, "pV%d" % i, [128, 512], F32) for i in range(2)]

            for i in range(4):
                P.dma("pool", lambda e, i=i: e.dma_start(
                    out=Wq[:, :, i * 512:(i + 1) * 512],
                    in_=w_peer_q[:, i * 512:(i + 1) * 512].rearrange("(c p) n -> p c n", p=128)), [], ["Wq"])
            sk32 = cs_[:].rearrange("p a (b c) -> p (a b) c", c=128)
            ld(sk32, sub_keys.rearrange("a k c -> k a c"), "cands")
            for hp in range(16):
                P.op("pe", lambda e, hp=hp: e.transpose(out=pQ[(hp // 4) % 2][:, hp % 4, :], in_=sk32[:, hp, :],
                                                        identity=identf[:]), ["cands", "identf"], ["pQ%d" % ((hp // 4) % 2)])
                if hp % 4 == 3:
                    P.op("act", lambda e, hp=hp: e.copy(out=skT[:, hp - 3:hp + 1, :], in_=pQ[(hp // 4) % 2][:]),
                         ["pQ%d" % ((hp // 4) % 2)], ["skT"])
            ld(gff[:], g_ffn.partition_broadcast(128), "gff")
            ld(iota[:], c_iota, "iota")
            ld(gfn[:], g_final.partition_broadcast(128), "gfn")

            def d_front(j):
                xt1 = x1b[j % 3]
                xk = "x1b%d" % (j % 3)
                ids = ids2[j % 3]
                idk = "ids%d" % (j % 3)
                hn32 = hn32b[j % 2]
                hnk = "hn32_%d" % (j % 2)
                gate = gateb[j % 2]
                gk = "gate%d" % (j % 2)
                P.dma("sp", lambda e: e.dma_start(out=xt1[:], in_=X1_d.ap()[j * 128:(j + 1) * 128, :]),
                      ["X1_d"], [xk])
                norm_tile(bufs, None, "gff", gff, hn2T[:], "hn2T", j, hn32=(hn32[:], hnk), skip_load=True,
                          xt_ovr=(xt1, xk))
                yield
                for hp in range(16):
                    b = (hp // 4) % 2
                    for c in range(8):
                        P.op("pe", lambda e, hp=hp, c=c, b=b: e.matmul(
                            pQ[b][:, hp % 4, :], lhsT=Wq[:, c, hp * 128:(hp + 1) * 128], rhs=hn2T[:, c, :],
                            start=(c == 0), stop=(c == 7)), ["Wq", "hn2T"], ["pQ%d" % b])
                    if hp % 4 == 3:
                        P.op("act", lambda e, hp=hp, b=b: e.copy(out=qT[:, hp - 3:hp + 1, :], in_=pQ[b][:]),
                             ["pQ%d" % b], ["qT"])
                    yield
                for hp in range(16):
                    b = (hp // 4) % 2
                    P.op("pe", lambda e, hp=hp, b=b: e.matmul(pS3[b][:, hp % 4, :], lhsT=qT[:, hp, :], rhs=skT[:, hp, :],
                                                              start=True, stop=True), ["qT", "skT"], ["pS3%d" % b])
                    if hp % 4 == 3:
                        P.op("act", lambda e, hp=hp, b=b: e.copy(out=S[:, hp - 3:hp + 1, :], in_=pS3[b][:]),
                             ["pS3%d" % b], ["S"])
                    yield
                for hp in range(16):
                    P.op("dve", lambda e, hp=hp: e.max(out=tv[:, hp, 0:8], in_=S[:, hp, :]), ["S"], ["tv"])
                    P.op("dve", lambda e, hp=hp: e.max_index(out=ti[:, hp, 0:8], in_max=tv[:, hp, 0:8], in_values=S[:, hp, :]),
                         ["S", "tv"], ["ti"])
                    yield
                    P.op("dve", lambda e, hp=hp: e.match_replace(out=S2[:], in_to_replace=tv[:, hp, 0:8],
                                                                 in_values=S[:, hp, :], imm_value=-3e38), ["S", "tv"], ["S2"])
                    P.op("dve", lambda e, hp=hp: e.max(out=tv[:, hp, 8:16], in_=S2[:]), ["S2"], ["tv"])
                    P.op("dve", lambda e, hp=hp: e.max_index(out=ti[:, hp, 8:16], in_max=tv[:, hp, 8:16], in_values=S2[:]),
                         ["S2", "tv"], ["ti"])
                    yield
                P.op("dve", lambda e: e.tensor_copy(out=tif[:], in_=ti[:]), ["ti"], ["tif"])
                tv4 = tv[:].rearrange("p (h two) k -> p h two k", two=2)
                tf4 = tif[:].rearrange("p (h two) k -> p h two k", two=2)
                P.op("dve", lambda e: e.tensor_scalar(out=ti0[:], in0=tf4[:, :, 0, :], scalar1=128.0, scalar2=None,
                                                      op0=ALU.mult), ["tif"], ["ti0"])
                yield
                P.op("dve", lambda e: e.tensor_tensor(
                    out=cs_[:].rearrange("p h (i j) -> p h i j", i=16),
                    in0=tv4[:, :, 0, :].unsqueeze(3).to_broadcast([128, 8, 16, 16]),
                    in1=tv4[:, :, 1, :].unsqueeze(2).to_broadcast([128, 8, 16, 16]), op=ALU.add), ["tv"], ["cands"])
                yield
                for hh in range(8):
                    P.op("dve", lambda e, hh=hh: e.max(out=bs[:, hh, 0:8], in_=cs_[:, hh, :]), ["cands"], ["bs"])
                    P.op("dve", lambda e, hh=hh: e.max_index(out=bpos[:, hh, 0:8], in_max=bs[:, hh, 0:8],
                                                             in_values=cs_[:, hh, :]), ["cands", "bs"], ["bpos"])
                    yield
                    P.op("dve", lambda e, hh=hh: e.match_replace(out=c2[:], in_to_replace=bs[:, hh, 0:8],
                                                                 in_values=cs_[:, hh, :], imm_value=-3e38),
                         ["cands", "bs"], ["c2"])
                    P.op("dve", lambda e, hh=hh: e.max(out=bs[:, hh, 8:16], in_=c2[:]), ["c2"], ["bs"])
                    yield
                    P.op("dve", lambda e, hh=hh: e.max_index(out=bpos[:, hh, 8:16], in_max=bs[:, hh, 8:16],
                                                             in_values=c2[:]), ["c2", "bs"], ["bpos"])
                    yield
                P.op("dve", lambda e: e.tensor_single_scalar(out=pu[:, 0], in_=bpos[:], scalar=4, op=ALU.logical_shift_right),
                     ["bpos"], ["pu"])
                P.op("dve", lambda e: e.tensor_single_scalar(out=pu[:, 1], in_=bpos[:], scalar=15, op=ALU.bitwise_and),
                     ["bpos"], ["pu"])
                P.op("dve", lambda e: e.tensor_copy(out=pf[:], in_=pu[:]), ["pu"], ["pf"])
                yield
                oh4 = cid[:].rearrange("p h (k c) -> p h k c", c=16)
                oh3 = cid[:].rearrange("p h (k c) -> p (h k) c", c=16)
                io4 = iota[:, 0:16].unsqueeze(1).unsqueeze(1).to_broadcast([128, 8, 16, 16])
                for w_ in range(2):
                    tbl = ti0[:] if w_ == 0 else tf4[:, :, 1, :]
                    tkey = "ti0" if w_ == 0 else "tif"
                    P.op("dve", lambda e, w_=w_: e.tensor_tensor(
                        out=oh4, in0=io4, in1=pf[:, w_].unsqueeze(3).to_broadcast([128, 8, 16, 16]), op=ALU.is_equal),
                        ["iota", "pf"], ["candid"])
                    yield
                    P.op("dve", lambda e, tbl=tbl: e.tensor_tensor(
                        out=oh4, in0=oh4, in1=tbl.unsqueeze(2).to_broadcast([128, 8, 16, 16]), op=ALU.mult),
                        ["candid", tkey], ["candid"])
                    yield
                    P.op("dve", lambda e, w_=w_: e.tensor_reduce(out=idp[:, w_, :], in_=oh3, axis=AX.X, op=ALU.add),
                         ["candid"], ["idp"])
                    yield
                P.op("dve", lambda e: e.tensor_tensor(out=idf[:], in0=idp[:, 0, :], in1=idp[:, 1, :], op=ALU.add),
                     ["idp"], ["idf"])
                P.op("dve", lambda e: e.tensor_copy(out=ids[:], in_=idf[:]), ["idf"], [idk])
                P.op("dve", lambda e: e.tensor_scalar(out=nb[:], in0=bs[:, :, 0], scalar1=-1.0, scalar2=None, op0=ALU.mult),
                     ["bs"], ["nb"])
                yield
                for hh in range(8):
                    P.op("act", lambda e, hh=hh: e.activation(out=eg[:, hh, :], in_=bs[:, hh, :], func=AF.Exp,
                                                              bias=nb[:, hh:hh + 1], scale=1.0, accum_out=zs_[:, hh:hh + 1]),
                         ["bs", "nb"], ["eg", "zs"])
                P.op("dve", lambda e: e.reciprocal(out=zs_[:], in_=zs_[:]), ["zs"], ["zs"])
                P.op("dve", lambda e: e.tensor_tensor(out=gate[:], in0=eg[:], in1=zs_[:].unsqueeze(2).to_broadcast([128, 8, 16]),
                                                      op=ALU.mult), ["eg", "zs"], [gk])
                yield

            uev = {}
            vev = {}

            def g_step(j, s_):
                ids = ids2[j % 3]
                idk = "ids%d" % (j % 3)
                hn32 = hn32b[j % 2]
                hnk = "hn32_%d" % (j % 2)
                gate = gateb[j % 2]
                gk = "gate%d" % (j % 2)
                b = s_ % NB
                uvb = UVb[b]
                uk = "UVb%d" % b
                d_ = dg[s_ % 4]
                dk = "dg%d" % (s_ % 4)
                dA = dgA[s_ % 4]
                dAk = "dgA%d" % (s_ % 4)
                ak = "actv%d" % (s_ % 8)
                gsk = "gsb%d" % (s_ % 8)
                if s_ >= NB and s_ % 4 == 0:
                    P._wait("pool", uev[(j, s_ - (NB - 3))])
                    P._wait("pool", vev[(j, s_ - (NB - 3))])
                P.dma("pool", lambda e: e.indirect_dma_start(
                    out=uvb[:], out_offset=None, in_=UV16_d.ap(),
                    in_offset=bass.IndirectOffsetOnAxis(ap=ids[:, s_:s_ + 1], axis=0)), [idk, "tbl16"], [uk],
                    own="G%d" % b)
                uev[(j, s_)] = P.op("dve", lambda e: e.scalar_tensor_tensor(
                    out=junk[:], in0=uvb[:, 0:D], scalar=1.0, in1=hn32[:], op0=ALU.mult, op1=ALU.mult,
                    accum_out=actv[:, s_:s_ + 1]), [uk, hnk], ["junk", ak])
                P.op("act", lambda e: e.activation(out=gsb[:, s_ % 8:s_ % 8 + 1], in_=actv[:, s_:s_ + 1], func=AF.Gelu),
                     [ak], [gsk])
                P.op("act", lambda e: e.activation(out=dA[:], in_=identb[:], func=AF.Copy,
                                                   scale=gsb[:, s_ % 8:s_ % 8 + 1]), ["identb", gsk], [dAk])
                P.op("act", lambda e: e.activation(out=d_[:], in_=dA[:], func=AF.Copy,
                                                   scale=gate[:].rearrange("p a b -> p (a b)")[:, s_:s_ + 1]), [dAk, gk], [dk])
                for nh in range(2):
                    vev[(j, s_)] = P.op("pe", lambda e, nh=nh: e.matmul(
                        pV[nh][:], lhsT=d_[:], rhs=uvb[:, D + nh * 512:D + (nh + 1) * 512], start=(s_ == 0), stop=(s_ == 127)),
                        [dk, uk], ["pV%d" % nh])

            def v_end(j):
                xt1 = x1b[j % 3]
                xk = "x1b%d" % (j % 3)
                for nh in range(2):
                    P.op("dve", lambda e, nh=nh: e.tensor_tensor(out=x2[:, nh * 512:(nh + 1) * 512], in0=pV[nh][:],
                                                                 in1=xt1[:, nh * 512:(nh + 1) * 512], op=ALU.add),
                         ["pV%d" % nh, xk], ["x2"])
                P.op("act", lambda e: e.activation(out=junk[:], in_=x2[:], func=AF.Square, accum_out=ss2[:]),
                     ["x2"], ["junk", "ss2"])
                P.op("act", lambda e: e.activation(out=rs2[:], in_=ss2[:], func=AF.Sqrt, scale=1.0 / D, bias=1e-6),
                     ["ss2"], ["rs2"])
                P.op("dve", lambda e: e.reciprocal(out=rs2[:], in_=rs2[:]), ["rs2"], ["rs2"])
                P.op("dve", lambda e: e.scalar_tensor_tensor(out=outt[:], in0=x2[:], scalar=rs2[:], in1=gfn[:],
                                                              op0=ALU.mult, op1=ALU.mult), ["x2", "rs2", "gfn"], ["outt"])
                stg(out[j * 128:(j + 1) * 128, :], outt[:], "outt", wkey="out")

            for _ in d_front(0):
                pass
            for j in range(NS):
                fr = d_front(j + 1) if j + 1 < NS else None
                for s_ in range(128):
                    g_step(j, s_)
                    if fr is not None:
                        for _ in range(2 if s_ % 3 == 0 else 1):
                            if next(fr, "done") == "done":
                                fr = None
                                break
                if fr is not None:
                    for _ in fr:
                        pass
                v_end(j)
            P.barrier()
            P.emit()
    return nc


def _make_in_maps(inputs):
    f = lambda k: np.ascontiguousarray(np.asarray(inputs[k], dtype=np.float32))
    x = f("x")
    mem = f("mem")
    w_in = f("w_in")[0]
    cols_a = np.concatenate([np.arange(1536, 1792), np.arange(1792, 2048), np.arange(2048, 2304),
                             np.arange(2560, 2816), np.arange(2304, 2560), np.arange(2816, 3072)])
    qcols = []
    for p in range(2):
        for r in range(4):
            for hh in (8 * p + r, 8 * p + 4 + r):
                qcols.append(512 + hh * 64 + np.arange(64))
    cols_b = np.concatenate([np.arange(0, 512)] + qcols + [np.arange(3072, 3120), np.arange(3120, 3632),
                                                          np.arange(3632, 6704)])
    w_a = np.ascontiguousarray(w_in[:, cols_a])
    w_b = np.ascontiguousarray(w_in[:, cols_b])
    shared = {
        "w_a": w_a, "w_b": w_b, "rel_bias": f("rel_bias"), "g_mix": f("g_mix").reshape(1, D),
        "w_pool_grp": f("w_pool_grp")[0], "pool_scale": f("pool_scale").reshape(4, 128),
        "w_pool_out": f("w_pool_out")[0], "w_cmp_k": f("w_cmp_k")[0], "w_cmp_v": f("w_cmp_v")[0],
        "pe_k": f("pe_k")[0], "pe_v": f("pe_v")[0], "w_nsa_out": f("w_nsa_out")[0],
        "g_mem": f("g_mem").reshape(1, D), "w_mem_kv": f("w_mem_kv")[0], "w_x_out": f("w_x_out")[0],
        "w_o": f("w_o")[0], "g_ffn": f("g_ffn").reshape(1, D), "w_peer_q": f("w_peer_q")[0],
        "sub_keys": f("peer_sub_keys")[0].reshape(16, 128, 128), "peer_u": f("peer_u")[0],
        "peer_v": f("peer_v")[0], "g_final": f("g_final").reshape(1, D),
    }
    consts = [_consts(0), _consts(1)]
    in_maps = []
    for c in range(8):
        b, h = c // 2, c % 2
        xt = x[b].reshape(32, 128, D)
        own = xt[h::2]
        if h == 1:
            prev = xt[0::2]
        else:
            prev = np.concatenate([np.zeros((1, 128, D), np.float32), xt[1::2][:15]], 0)
        m = dict(shared)
        m["x_all"] = np.ascontiguousarray(x[b])
        m["x_own"] = np.ascontiguousarray(own.reshape(NS * 128, D))
        m["x_prev"] = np.ascontiguousarray(prev.reshape(NS * 128, D))
        m["mem"] = np.ascontiguousarray(mem[b])
        for k, v in consts[h].items():
            m["c_" + k] = v
        in_maps.append(m)
    return in_maps


def _assemble(res, key="out"):
    outf = np.zeros((4, 32, 128, D), np.float32)
    for c in range(8):
        b, h = c // 2, c % 2
        outf[b, h::2] = np.asarray(res.results[c][key]).reshape(NS, 128, D)
    return outf.reshape(4, T, D)


def kernel(**inputs):
    in_maps = _make_in_maps(inputs)
    nc = build_program()
    res = run_bass_kernel_spmd(nc, in_maps, core_ids=list(range(8)))
    return _assemble(res)
```

```python
import math
from contextlib import ExitStack
import numpy as np
import ml_dtypes
import concourse.bass as bass
import concourse.mybir as mybir
from concourse.bass_utils import run_bass_kernel_spmd

F32 = mybir.dt.float32
BF16 = mybir.dt.bfloat16
U32 = mybir.dt.uint32
AF = mybir.ActivationFunctionType
ALU = mybir.AluOpType
AX = mybir.AxisListType

import os
NDMA = 24
import os
C2STOP = 99
NEG = -30000.0
NS = 16
T = 4096
D = 1024


class Prog:
    CE = ("pe", "dve", "act", "pool")

    def __init__(self, nc, stack):
        self.nc = nc
        self.stack = stack
        self.own_sem = {}
        self.eng = {"pe": nc.tensor, "dve": nc.vector, "act": nc.scalar,
                    "pool": nc.gpsimd, "sp": nc.sync}
        self.stream = {e: [] for e in self.eng}
        self.count = {e: 0 for e in self.CE}
        self.sem = {e: stack.enter_context(nc.semaphore("s_" + e)) for e in self.CE}
        self.dsem = [stack.enter_context(nc.semaphore("s_dma%d" % i)) for i in range(NDMA)]
        self.gsem = [stack.enter_context(nc.semaphore("s_gdma%d" % i)) for i in range(NDMA)]
        self.dma_k = 0
        self.gdma_k = 0
        self.last_w = {}
        self.readers = {}
        self.waited = {e: {} for e in self.eng}
        self.n_ops = 0

    def _wait(self, e, ev):
        sem, val, src = ev
        if src == "pe" and e == "pe":
            return
        key = id(sem)
        if self.waited[e].get(key, 0) >= val:
            return
        self.waited[e][key] = val
        self.stream[e].append(lambda eng, s=sem, v=val: eng.wait_ge(s, v))

    def _deps(self, reads, writes):
        evs = []
        for k in reads:
            if k in self.last_w:
                evs.append(self.last_w[k])
        for k in writes:
            if k in self.last_w:
                evs.append(self.last_w[k])
            evs.extend(self.readers.get(k, []))
        return evs

    def _commit(self, ev, reads, writes):
        for k in reads:
            self.readers.setdefault(k, []).append(ev)
        for k in writes:
            self.last_w[k] = ev
            self.readers[k] = []

    def op(self, e, fn, reads=(), writes=()):
        for ev in self._deps(reads, writes):
            self._wait(e, ev)
        self.count[e] += 1
        sem = self.sem[e]
        self.stream[e].append(lambda eng, f=fn, s=sem: f(eng).then_inc(s, 1))
        ev = (sem, self.count[e], e)
        self._commit(ev, reads, writes)
        self.n_ops += 1
        return ev

    def dma(self, q, fn, reads=(), writes=(), own=None):
        for ev in self._deps(reads, writes):
            self._wait(q, ev)
        if own is not None:
            if own not in self.own_sem:
                self.own_sem[own] = [self.stack.enter_context(self.nc.semaphore("s_own_%s" % own)), 0]
            rec = self.own_sem[own]
            rec[1] += 1
            sem = rec[0]
            self.stream[q].append(lambda eng, f=fn, s=sem: f(eng).then_inc(s, 16))
            ev = (sem, 16 * rec[1], "dma")
            self._commit(ev, reads, writes)
            self.n_ops += 1
            return ev
        if q == "pool":
            k = self.gdma_k
            self.gdma_k += 1
            sem = self.gsem[k % NDMA]
        else:
            k = self.dma_k
            self.dma_k += 1
            sem = self.dsem[k % NDMA]
        rnd = k // NDMA
        if rnd > 0:
            self._wait(q, (sem, 16 * rnd, "dma"))
        self.stream[q].append(lambda eng, f=fn, s=sem: f(eng).then_inc(s, 16))
        ev = (sem, 16 * (rnd + 1), "dma")
        self._commit(ev, reads, writes)
        self.n_ops += 1
        return ev

    def barrier(self):
        evs = [(self.sem[e], self.count[e], e) for e in self.CE if self.count[e] > 0]
        for i in range(NDMA):
            n = (self.dma_k - i + NDMA - 1) // NDMA if self.dma_k > i else 0
            if n > 0:
                evs.append((self.dsem[i], 16 * n, "dma"))
        for i in range(NDMA):
            n = (self.gdma_k - i + NDMA - 1) // NDMA if self.gdma_k > i else 0
            if n > 0:
                evs.append((self.gsem[i], 16 * n, "dma"))
        for rec in self.own_sem.values():
            if rec[1] > 0:
                evs.append((rec[0], 16 * rec[1], "dma"))
        for e in self.eng:
            for sem, val, src in evs:
                key = id(sem)
                if self.waited[e].get(key, 0) >= val:
                    continue
                self.waited[e][key] = val
                self.stream[e].append(lambda eng, s=sem, v=val: eng.wait_ge(s, v))

    def emit(self):
        nc = self.nc
        st = self.stream
        with nc.Block() as block:
            @block.tensor
            def _(eng):
                for f in st["pe"]:
                    f(eng)

            @block.vector
            def _(eng):
                for f in st["dve"]:
                    f(eng)

            @block.scalar
            def _(eng):
                for f in st["act"]:
                    f(eng)

            @block.gpsimd
            def _(eng):
                for f in st["pool"]:
                    f(eng)

            @block.sync
            def _(eng):
                for f in st["sp"]:
                    f(eng)
        self.stream = {e: [] for e in self.eng}


def _t5_bucket(n):
    n = np.maximum(n, 0)
    nf = np.maximum(n, 1).astype(np.float32)
    large = 16 + (np.log(nf / np.float32(16)) / np.float32(math.log(8.0)) * np.float32(16)).astype(np.int32)
    large = np.minimum(large, 31)
    return np.where(n < 16, n, large)


def _oh_rows(dist, valid, minus31):
    L = dist.shape[0]
    oh = np.zeros((33, L), np.float32)
    b = _t5_bucket(dist)
    idx = np.arange(L)
    oh[b[valid], idx[valid]] = 1.0
    if minus31:
        oh[31, idx[valid]] -= 1.0
    oh[32, ~valid] = NEG
    return oh


def _consts(h):
    c = {}
    c["ident"] = np.eye(128, dtype=np.float32)
    c["jflip"] = np.eye(128, dtype=np.float32)[::-1].copy()
    ex = np.zeros((65, 32, 128), np.float32)
    for kt in range(32):
        ex[2 * kt, kt, :64] = 1.0
        ex[2 * kt + 1, kt, 64:] = 1.0
    ex[64] = 1.0
    c["expand"] = ex.reshape(65, 32 * 128)[0:64].astype(ml_dtypes.bfloat16)
    cs = np.arange(256) * 16
    ss = np.arange(64) * 64
    ov = ((cs[:, None] < ss[None, :] + 64) & (cs[:, None] + 32 > ss[None, :])).astype(np.float32)
    ov[255] = 0.0
    c["overlap"] = ov.reshape(2, 128, 64).transpose(1, 0, 2).reshape(128, 128).copy()
    i = np.arange(256)

    def typ(delta, win, minus31):
        if delta is None:
            return _oh_rows(np.zeros(256, np.int64), np.zeros(256, bool), minus31)
        dist = i - 127 + 128 * delta
        valid = dist >= 0
        if win:
            valid &= dist < 512
        return _oh_rows(dist, valid, minus31)
    if h == 0:
        sel = [typ(1, False, True), typ(0, False, True), typ(None, False, True)]
        win = [typ(4, True, False), typ(3, True, False), typ(2, True, False), typ(1, True, False),
               typ(0, True, False), typ(None, True, False)]
    else:
        sel = [typ(2, False, True), typ(1, False, True), typ(0, False, True)]
        win = [typ(None, True, False), typ(4, True, False), typ(3, True, False), typ(2, True, False),
               typ(1, True, False), typ(0, True, False)]
    c["ohw"] = np.concatenate(sel + win, axis=1)
    ii = np.arange(8192)
    dist = ii - 4111 + 128 * h
    c["ohc"] = _oh_rows(dist, dist >= 0, False)
    cm = np.zeros((NS, 128, 64), np.float32)
    ad = np.zeros((NS, 128, 64), np.float32)
    s = np.arange(64)[None, :]
    for j in range(NS):
        t = 128 * (2 * j + h) + np.arange(128)
        bq = (t // 64)[:, None]
        forced = (s == 0) | (s == bq) | (s == bq - 1)
        causal = s <= bq
        cm[j] = (causal & ~forced)
        ad[j] = np.where(forced, 1e6, np.where(causal, 0.0, -1e6))
    c["cm"] = cm.transpose(1, 0, 2).reshape(128, NS * 64).copy()
    c["addt"] = ad.transpose(1, 0, 2).reshape(128, NS * 64).copy()
    selm = np.zeros((48, 48, 64), np.float32)
    for a in range(48):
        selm[a, a, :] = 1.0
    c["gsel"] = selm.reshape(48, 48 * 64).astype(ml_dtypes.bfloat16)
    def amat(first):
        cur = np.zeros((4, 128, 128), np.float32)
        prv = np.zeros((4, 128, 128), np.float32)
        for g, w in enumerate((2, 4, 8, 16)):
            for t in range(128):
                cnt = min(t + 1, w) if first else w
                for d in range(w):
                    tp = t - d
                    if tp >= 0:
                        cur[g, tp, t] += 1.0 / cnt
                    elif not first:
                        prv[g, 128 + tp, t] += 1.0 / cnt
                cur[g, t, t] -= 1.0
        return cur, prv
    gc, gp = amat(False)
    fc, fp = amat(h == 0)
    am = np.stack([gc, gp, fc, fp], 0)
    c["amat"] = am.transpose(2, 0, 1, 3).reshape(128, 16 * 128).copy()
    c["iota"] = np.tile(np.arange(256, dtype=np.float32)[None, :], (128, 1))
    return c


def build_program(dbg=False, with_peer=True, stop_after=None):
    nc = bass.Bass("TRN2", target_bir_lowering=False)

    def din(name, shape, dt=F32):
        return nc.dram_tensor(name, list(shape), dt, kind="ExternalInput")

    def dscr(name, shape, dt, out=False):
        if out:
            return nc.dram_tensor(name, list(shape), dt, kind="ExternalOutput")
        return nc.dram_tensor(name, list(shape), dt)

    x_all = din("x_all", [T, D]).ap()
    x_own = din("x_own", [NS * 128, D]).ap()
    x_prev = din("x_prev", [NS * 128, D]).ap()
    mem = din("mem", [256, D]).ap()
    w_a = din("w_a", [D, 1536]).ap()
    w_b = din("w_b", [D, 5168]).ap()
    rel_bias = din("rel_bias", [32, 16]).ap()
    g_mix = din("g_mix", [1, D]).ap()
    w_pool_grp = din("w_pool_grp", [4, 128, 128]).ap()
    pool_scale = din("pool_scale", [4, 128]).ap()
    w_pool_out = din("w_pool_out", [512, D]).ap()
    w_cmp_k = din("w_cmp_k", [2048, 64]).ap()
    w_cmp_v = din("w_cmp_v", [2048, 64]).ap()
    pe_k = din("pe_k", [32, 64]).ap()
    pe_v = din("pe_v", [32, 64]).ap()
    w_nsa_out = din("w_nsa_out", [D, D]).ap()
    g_mem = din("g_mem", [1, D]).ap()
    w_mem_kv = din("w_mem_kv", [D, D]).ap()
    w_x_out = din("w_x_out", [512, D]).ap()
    w_o = din("w_o", [D, D]).ap()
    g_ffn = din("g_ffn", [1, D]).ap()
    w_peer_q = din("w_peer_q", [D, 2048]).ap()
    sub_keys = din("sub_keys", [16, 128, 128]).ap()
    peer_u = din("peer_u", [16384, D]).ap()
    peer_v = din("peer_v", [16384, D]).ap()
    g_final = din("g_final", [1, D]).ap()
    c_ident = din("c_ident", [128, 128]).ap()
    c_jflip = din("c_jflip", [128, 128]).ap()
    c_expand = din("c_expand", [64, 4096], BF16).ap()
    c_overlap = din("c_overlap", [128, 128]).ap()
    c_ohw = din("c_ohw", [33, 2304]).ap()
    c_ohc = din("c_ohc", [33, 8192]).ap()
    c_cm = din("c_cm", [128, NS * 64]).ap()
    c_addt = din("c_addt", [128, NS * 64]).ap()
    c_gsel = din("c_gsel", [48, 3072], BF16).ap()
    c_amat = din("c_amat", [128, 2048]).ap()
    c_iota = din("c_iota", [128, 256]).ap()

    out = nc.dram_tensor("out", [NS * 128, D], F32, kind="ExternalOutput").ap()

    KT_d = dscr("KT_d", [4, 128, T], BF16)
    V_d = dscr("V_d", [32, 128, 512], BF16)
    QT_d = dscr("QT_d", [8, 128, NS * 128], BF16)
    QX_d = dscr("QX_d", [4, 128, NS * 128], BF16)
    GM_d = dscr("GM_d", [24, 128, NS * 128], BF16)
    YP_d = dscr("YP_d", [NS, 128, 1024], BF16)
    ON_d = dscr("ON_d", [NS, 128, 1024], BF16)
    X1_d = dscr("X1_d", [NS * 128, D], F32, out=dbg)
    fw_d = dscr("fw_d", [16, 2304], BF16)
    fc_d = dscr("fc_d", [16, 8192], BF16)
    UV16_d = dscr("UV16_d", [16384, 2 * D], BF16)

    def dap(h, offset, ap):
        return bass.AP(tensor=h, offset=offset, ap=ap)

    with ExitStack() as st0:
        P = Prog(nc, st0)

        uid = [0]

        def sb(stk, name, shape, dt):
            uid[0] += 1
            return stk.enter_context(nc.sbuf_tensor("%s_%d" % (name, uid[0]), list(shape), dt))

        def ps(stk, name, shape, dt=F32):
            uid[0] += 1
            return stk.enter_context(nc.psum_tensor("%s_%d" % (name, uid[0]), list(shape), dt))

        def ld(dst, src, key, q="sp", reads=()):
            P.dma(q, lambda e: e.dma_start(out=dst, in_=src), reads=list(reads), writes=[key])

        def stg(dst, src, key, q="sp", wkey=None):
            P.dma(q, lambda e: e.dma_start(out=dst, in_=src), reads=[key], writes=[wkey] if wkey else [])

        identf = sb(st0, "identf", [128, 128], F32)
        identb = sb(st0, "identb", [128, 128], BF16)
        jb = sb(st0, "jb", [128, 128], BF16)
        onesb = sb(st0, "onesb", [128, 128], BF16)
        GT = sb(st0, "GT", [48, NS * 128], BF16)
        kcT = sb(st0, "kcT", [128, 2, 256], BF16)
        vc = sb(st0, "vc", [128, 2, 4, 64], BF16)
        ctmp = sb(st0, "ctmp", [128, 128], F32)

        ld(identf[:], c_ident, "identf")
        P.op("dve", lambda e: e.tensor_copy(out=identb[:], in_=identf[:]), ["identf"], ["identb"])
        ld(ctmp[:], c_jflip, "ctmp")
        P.op("dve", lambda e: e.tensor_copy(out=jb[:], in_=ctmp[:]), ["ctmp"], ["jb"])
        P.op("dve", lambda e: e.memset(onesb[:], 1.0), [], ["onesb"])
        P.op("dve", lambda e: e.memset(kcT[:], 0.0), [], ["kcT"])
        P.op("dve", lambda e: e.memset(vc[:], 0.0), [], ["vc"])

        def norm_tile(stk_bufs, src_ap, gtile_key, gtile, dstT, dst_key, i, hn32=None, skip_load=False, xt_ovr=None):
            xt, sq, ss, rs, hn, pT = stk_bufs[i % 2]
            k = str(i % 2)
            xk_ = "xt" + k
            if xt_ovr is not None:
                xt, xk_ = xt_ovr
            if not skip_load:
                ld(xt[:], src_ap, xk_)
            P.op("act", lambda e: e.activation(out=sq[:], in_=xt[:], func=AF.Square, accum_out=ss[:]),
                 [xk_], ["sq" + k, "ss" + k])
            P.op("act", lambda e: e.activation(out=rs[:], in_=ss[:], func=AF.Sqrt, scale=1.0 / D, bias=1e-6),
                 ["ss" + k], ["rs" + k])
            P.op("dve", lambda e: e.reciprocal(out=rs[:], in_=rs[:]), ["rs" + k], ["rs" + k])
            if hn32 is not None:
                P.op("dve", lambda e: e.scalar_tensor_tensor(out=hn32[0], in0=xt[:], scalar=rs[:], in1=gtile[:],
                                                              op0=ALU.mult, op1=ALU.mult),
                     [xk_, "rs" + k, gtile_key], [hn32[1]])
                P.op("act", lambda e: e.copy(out=hn[:], in_=hn32[0]), [hn32[1]], ["hn" + k])
            else:
                P.op("dve", lambda e: e.scalar_tensor_tensor(out=hn[:], in0=xt[:], scalar=rs[:], in1=gtile[:],
                                                              op0=ALU.mult, op1=ALU.mult),
                     [xk_, "rs" + k, gtile_key], ["hn" + k])
            for c in range(8):
                P.op("pe", lambda e, c=c: e.transpose(out=pT[:, c, :], in_=hn[:, c * 128:(c + 1) * 128],
                                                       identity=identb[:]),
                     ["hn" + k, "identb"], ["pT" + k])
            P.op("act", lambda e: e.copy(out=dstT, in_=pT[:]), ["pT" + k], [dst_key])

        def norm_bufs(stk, with_xt=True):
            bufs = []
            for i in range(2):
                bufs.append((sb(stk, "xt%d" % i, [128, D], F32) if with_xt else None, sb(stk, "sq%d" % i, [128, D], F32),
                             sb(stk, "ss%d" % i, [128, 1], F32), sb(stk, "rs%d" % i, [128, 1], F32),
                             sb(stk, "hn%d" % i, [128, D], BF16), ps(stk, "pT%d" % i, [128, 8, 128], BF16)))
            return bufs

        def load_w(wbf, src_cols_ap, ncols, key):
            P.dma("pool", lambda e: e.dma_start(out=wbf[:, :, 0:ncols],
                                                in_=src_cols_ap.rearrange("(c p) n -> p c n", p=128)), [], [key])

        with ExitStack() as s1:
            relb = sb(s1, "relb", [33, 16], F32)
            ohw = sb(s1, "ohw", [33, 2304], F32)
            ohc = sb(s1, "ohc", [33, 8192], F32)
            fsb = sb(s1, "fsb", [16, 8192], BF16)
            fwb = sb(s1, "fwb", [16, 2304], BF16)
            pf = [ps(s1, "pf%d" % i, [16, 512], F32) for i in range(2)]
            P.op("dve", lambda e: e.memset(relb[:], 1.0), [], ["relb"])
            ld(relb[0:32, :], rel_bias, "relb")
            ld(ohw[:], c_ohw, "ohw")
            ld(ohc[:], c_ohc, "ohc")
            n = 0
            for i in range(5):
                w_ = 512 if i < 4 else 256
                pp = pf[n % 2]
                P.op("pe", lambda e, i=i, w_=w_, pp=pp: e.matmul(pp[:, 0:w_], lhsT=relb[:], rhs=ohw[:, i * 512:i * 512 + w_],
                                                                 start=True, stop=True),
                     ["relb", "ohw"], ["pf%d" % (n % 2)])
                P.op("act", lambda e, i=i, w_=w_, pp=pp: e.copy(out=fwb[:, i * 512:i * 512 + w_], in_=pp[:, 0:w_]),
                     ["pf%d" % (n % 2)], ["fwb"])
                n += 1
            for i in range(16):
                pp = pf[n % 2]
                P.op("pe", lambda e, i=i, pp=pp: e.matmul(pp[:], lhsT=relb[:], rhs=ohc[:, i * 512:(i + 1) * 512],
                                                          start=True, stop=True),
                     ["relb", "ohc"], ["pf%d" % (n % 2)])
                P.op("act", lambda e, i=i, pp=pp: e.copy(out=fsb[:, i * 512:(i + 1) * 512], in_=pp[:]),
                     ["pf%d" % (n % 2)], ["fsb"])
                n += 1
            stg(fw_d.ap(), fwb[:], "fwb", wkey="fw_d")
            stg(fc_d.ap(), fsb[:], "fsb", wkey="fc_d")
            P.barrier()
            P.emit()

        if stop_after == 'S':
            return nc
        with ExitStack() as sa:
            hnT = sb(sa, "hnT_all", [128, 8, T], BF16)
            XTc = sb(sa, "XTc", [128, 4, T], BF16)
            gt = sb(sa, "gt_a", [128, D], F32)
            bufs = norm_bufs(sa)
            wbfs = [sb(sa, "wbf%d" % i, [128, 8, 512], BF16) for i in range(2)]
            est = [sb(sa, "est%d" % i, [128, 512], BF16) for i in range(2)]
            pp = [ps(sa, "ppa%d" % i, [128, 512], F32) for i in range(2)]
            ld(gt[:], g_mix.partition_broadcast(128), "gt_a")
            for i in range(32):
                norm_tile(bufs, x_all[i * 128:(i + 1) * 128, :], "gt_a", gt, hnT[:, :, i * 128:(i + 1) * 128],
                          "hnT_all", i)
            n = 0
            load_w(wbfs[0], w_a[:, 0:512], 512, "wbf0")
            for piece in range(3):
                wbf = wbfs[piece % 2]
                wk = "wbf%d" % (piece % 2)
                if piece + 1 < 3:
                    load_w(wbfs[(piece + 1) % 2], w_a[:, (piece + 1) * 512:(piece + 2) * 512], 512, "wbf%d" % ((piece + 1) % 2))
                if piece < 2:
                    for blk in range(4):
                        for tb in range(8):
                            pq = pp[n % 2]
                            for c in range(8):
                                P.op("pe", lambda e, c=c, blk=blk, tb=tb, pq=pq, wbf=wbf: e.matmul(
                                    pq[:], lhsT=wbf[:, c, blk * 128:(blk + 1) * 128],
                                    rhs=hnT[:, c, tb * 512:(tb + 1) * 512], start=(c == 0), stop=(c == 7)),
                                    [wk, "hnT_all"], ["ppa%d" % (n % 2)])
                            if piece == 0:
                                P.op("act", lambda e, blk=blk, tb=tb, pq=pq: e.copy(
                                    out=XTc[:, blk, tb * 512:(tb + 1) * 512], in_=pq[:]),
                                    ["ppa%d" % (n % 2)], ["XTc"])
                            else:
                                es = est[n % 2]
                                P.op("act", lambda e, es=es, pq=pq: e.copy(out=es[:], in_=pq[:]),
                                     ["ppa%d" % (n % 2)], ["est%d" % (n % 2)])
                                stg(KT_d.ap()[blk, :, tb * 512:(tb + 1) * 512], es[:], "est%d" % (n % 2), wkey="KT_d")
                            n += 1
                else:
                    for i in range(32):
                        pq = pp[n % 2]
                        for c in range(8):
                            P.op("pe", lambda e, c=c, i=i, pq=pq, wbf=wbf: e.matmul(
                                pq[:], lhsT=hnT[:, c, i * 128:(i + 1) * 128], rhs=wbf[:, c, :],
                                start=(c == 0), stop=(c == 7)),
                                [wk, "hnT_all"], ["ppa%d" % (n % 2)])
                        es = est[n % 2]
                        P.op("act", lambda e, es=es, pq=pq: e.copy(out=es[:], in_=pq[:]),
                             ["ppa%d" % (n % 2)], ["est%d" % (n % 2)])
                        stg(V_d.ap()[i], es[:], "est%d" % (n % 2), wkey="V_d")
                        n += 1
            wl32 = sb(sa, "wl32", [128, 32, 64], F32)
            WkL = sb(sa, "WkL", [128, 32, 64], BF16)
            WvL = sb(sa, "WvL", [128, 32, 64], BF16)
            pe32 = sb(sa, "pe32", [128, 2, 32], F32)
            peT = sb(sa, "peT", [128, 2, 32], BF16)
            cK = sb(sa, "cK", [128, 1], F32)
            cV = sb(sa, "cV", [1, 64], BF16)
            pcK = ps(sa, "pcK", [128, 8], F32)
            pcV = ps(sa, "pcV", [1, 64], F32)
            pkc = ps(sa, "pkc", [128, 256], F32)
            pvc = ps(sa, "pvc", [128, 256], F32)
            for half in range(2):
                ld(wl32[half * 64:(half + 1) * 64, :, :], w_cmp_k.rearrange("(l d) o -> d l o", d=64), "wl32")
            P.op("dve", lambda e: e.tensor_copy(out=WkL[:], in_=wl32[:]), ["wl32"], ["WkL"])
            for half in range(2):
                ld(wl32[half * 64:(half + 1) * 64, :, :], w_cmp_v.rearrange("(l d) o -> d l o", d=64), "wl32")
            P.op("dve", lambda e: e.tensor_copy(out=WvL[:], in_=wl32[:]), ["wl32"], ["WvL"])
            for half in range(2):
                P.dma("sp", lambda e, half=half: e.dma_start(out=pe32[half * 64:(half + 1) * 64, 0, :],
                                                             in_=pe_k.rearrange("l d -> d l"),
                                                             allow_slow_non_contiguous=True), [], ["pe32"])
                P.dma("sp", lambda e, half=half: e.dma_start(out=pe32[half * 64:(half + 1) * 64, 1, :],
                                                             in_=pe_v.rearrange("l d -> d l"),
                                                             allow_slow_non_contiguous=True), [], ["pe32"])
            P.op("dve", lambda e: e.tensor_copy(out=peT[:], in_=pe32[:]), ["pe32"], ["peT"])
            for half in range(2):
                hs = slice(half * 64, (half + 1) * 64)
                for l in range(32):
                    P.op("pe", lambda e, hs=hs, l=l: e.matmul(pcK[hs, 0:1], lhsT=WkL[hs, l, :], rhs=peT[hs, 0, l:l + 1],
                                                              start=(l == 0), stop=(l == 31)),
                         ["WkL", "peT"], ["pcK"])
            P.op("act", lambda e: e.copy(out=cK[:], in_=pcK[:, 0:1]), ["pcK"], ["cK"])
            for l in range(32):
                P.op("pe", lambda e, l=l: e.matmul(pcV[:], lhsT=peT[0:64, 1, l:l + 1], rhs=WvL[0:64, l, :],
                                                   start=(l == 0), stop=(l == 31)),
                     ["WvL", "peT"], ["pcV"])
            P.op("act", lambda e: e.copy(out=cV[:], in_=pcV[:]), ["pcV"], ["cV"])

            def tokview(blk, hs, l, c0, m):
                v = XTc[hs, blk, :].rearrange("p (c s) -> p c s", s=16)
                return v[:, c0 + l // 16:c0 + l // 16 + m, l % 16]

            for g in range(4):
                hs = slice((g % 2) * 64, (g % 2) * 64 + 64)
                for l in range(32):
                    P.op("pe", lambda e, hs=hs, l=l, g=g: e.matmul(
                        pkc[hs, 0:255], lhsT=WkL[hs, l, :], rhs=tokview(g // 2, hs, l, 0, 255),
                        start=(l == 0), stop=(l == 31)), ["WkL", "XTc"], ["pkc"])
                P.op("act", lambda e, hs=hs, g=g: e.activation(out=kcT[hs, g // 2, 0:255], in_=pkc[hs, 0:255],
                                                               func=AF.Identity, bias=cK[hs, :], scale=1.0),
                     ["pkc", "cK"], ["kcT"])
            for ct in range(2):
                m = 128 if ct == 0 else 127
                for g in range(4):
                    hs = slice((g % 2) * 64, (g % 2) * 64 + 64)
                    for l in range(32):
                        P.op("pe", lambda e, hs=hs, l=l, g=g, ct=ct, m=m: e.matmul(
                            pvc[0:m, g * 64:(g + 1) * 64], lhsT=tokview(2 + g // 2, hs, l, ct * 128, m),
                            rhs=WvL[hs, l, :], start=(l == 0), stop=False), ["WvL", "XTc"], ["pvc"])
                    P.op("pe", lambda e, g=g, m=m: e.matmul(pvc[0:m, g * 64:(g + 1) * 64], lhsT=onesb[0:1, 0:m],
                                                            rhs=cV[:], start=False, stop=True),
                         ["onesb", "cV"], ["pvc"])
                P.op("act", lambda e, ct=ct, m=m: e.copy(out=vc[0:m, ct, :, :], in_=pvc[0:m, :]), ["pvc"], ["vc"])
            P.barrier()
            P.emit()

        if stop_after == 'A':
            return nc
        with ExitStack() as sbk:
            hnT = sb(sbk, "hnT_own", [128, 8, NS * 128], BF16)
            hnTp = sb(sbk, "hnT_prev", [128, 8, NS * 128], BF16)
            gt = sb(sbk, "gt_b", [128, D], F32)
            bufs = norm_bufs(sbk)
            wbfs = [sb(sbk, "wbfb%d" % i, [128, 8, 512], BF16) for i in range(2)]
            est = [sb(sbk, "estb%d" % i, [128, 512], BF16) for i in range(2)]
            pp = [ps(sbk, "ppb%d" % i, [128, 512], F32) for i in range(2)]
            ld(gt[:], g_mix.partition_broadcast(128), "gt_b")
            for i in range(NS):
                norm_tile(bufs, x_own[i * 128:(i + 1) * 128, :], "gt_b", gt, hnT[:, :, i * 128:(i + 1) * 128],
                          "hnT_own", i)
            for i in range(NS):
                norm_tile(bufs, x_prev[i * 128:(i + 1) * 128, :], "gt_b", gt, hnTp[:, :, i * 128:(i + 1) * 128],
                          "hnT_prev", i)
            amat32 = sb(sbk, "amat32", [128, 2048], F32)
            amat = sb(sbk, "amat", [128, 16, 128], BF16)
            wg32 = sb(sbk, "wg32", [128, 4, 128], F32)
            wgb = sb(sbk, "wgb", [128, 4, 128], BF16)
            psc = sb(sbk, "psc", [128, 4], F32)
            wpo32 = sb(sbk, "wpo32", [128, 4, D], F32)
            wpo = sb(sbk, "wpo", [128, 4, D], BF16)
            uo = [sb(sbk, "uo%d" % i, [128, 512], BF16) for i in range(2)]
            up = [sb(sbk, "up%d" % i, [128, 512], BF16) for i in range(2)]
            pooledT = sb(sbk, "pooledT", [128, 4, 128], BF16)
            mixedT = sb(sbk, "mixedT", [128, 4, 128], BF16)
            ypst = [sb(sbk, "ypst%d" % i, [128, D], BF16) for i in range(2)]
            ppl = ps(sbk, "ppl", [128, 4, 128], F32)
            pmx = ps(sbk, "pmx", [128, 4, 128], F32)
            pyp = [ps(sbk, "pyp%d" % i, [128, 4, 128], F32) for i in range(2)]
            ld(amat32[:], c_amat, "amat32")
            P.op("dve", lambda e: e.tensor_copy(out=amat[:].rearrange("p a b -> p (a b)"), in_=amat32[:]),
                 ["amat32"], ["amat"])
            ld(wg32[:], w_pool_grp.rearrange("g c d -> c g d"), "wg32")
            P.op("dve", lambda e: e.tensor_copy(out=wgb[:], in_=wg32[:]), ["wg32"], ["wgb"])
            P.dma("sp", lambda e: e.dma_start(out=psc[:], in_=pool_scale.rearrange("g d -> d g"),
                                               allow_slow_non_contiguous=True), [], ["psc"])
            ld(wpo32[:], w_pool_out.rearrange("(g p) n -> p g n", p=128), "wpo32")
            P.op("pool", lambda e: e.tensor_copy(out=wpo[:], in_=wpo32[:]), ["wpo32"], ["wpo"])
            load_w(wbfs[0], w_b[:, 0:512], 512, "wbfb0")
            load_w(wbfs[1], w_b[:, 512:1024], 512, "wbfb1")
            wbf = wbfs[0]
            n = 0
            for j in range(NS):
                k = j % 2
                for (src, dst, nm) in ((hnT, uo[k], "uo%d" % k), (hnTp, up[k], "up%d" % k)):
                    pq = pp[n % 2]
                    for c in range(8):
                        P.op("pe", lambda e, c=c, j=j, pq=pq, src=src, wbf=wbf: e.matmul(
                            pq[:], lhsT=src[:, c, j * 128:(j + 1) * 128], rhs=wbf[:, c, :],
                            start=(c == 0), stop=(c == 7)), ["wbfb0", "hnT_own", "hnT_prev"], ["ppb%d" % (n % 2)])
                    P.op("act", lambda e, dst=dst, pq=pq: e.copy(out=dst[:], in_=pq[:]), ["ppb%d" % (n % 2)], [nm])
                    n += 1
                kind = 2 if j == 0 else 0
                for g in range(4):
                    P.op("pe", lambda e, g=g, k=k, kind=kind: e.matmul(
                        ppl[:, g, :], lhsT=uo[k][:, g * 128:(g + 1) * 128], rhs=amat[:, kind * 4 + g, :],
                        start=True, stop=False), ["uo%d" % k, "amat"], ["ppl"])
                    P.op("pe", lambda e, g=g, k=k, kind=kind: e.matmul(
                        ppl[:, g, :], lhsT=up[k][:, g * 128:(g + 1) * 128], rhs=amat[:, (kind + 1) * 4 + g, :],
                        start=False, stop=True), ["up%d" % k, "amat"], ["ppl"])
                P.op("dve", lambda e: e.tensor_copy(out=pooledT[:], in_=ppl[:]), ["ppl"], ["pooledT"])
                for g in range(4):
                    P.op("pe", lambda e, g=g: e.matmul(pmx[:, g, :], lhsT=wgb[:, g, :], rhs=pooledT[:, g, :],
                                                       start=True, stop=True), ["wgb", "pooledT"], ["pmx"])
                for g in range(4):
                    P.op("act", lambda e, g=g: e.activation(out=mixedT[:, g, :], in_=pmx[:, g, :], func=AF.Copy,
                                                            scale=psc[:, g:g + 1]), ["pmx", "psc"], ["mixedT"])
                for hf in range(2):
                    for m in range(4):
                        mm = hf * 4 + m
                        for g in range(4):
                            P.op("pe", lambda e, g=g, m=m, mm=mm, hf=hf: e.matmul(
                                pyp[hf][:, m, :], lhsT=wpo[:, g, mm * 128:(mm + 1) * 128], rhs=mixedT[:, g, :],
                                start=(g == 0), stop=(g == 3)), ["wpo", "mixedT"], ["pyp%d" % hf])
                    P.op("dve", lambda e, hf=hf, k=k: e.tensor_copy(
                        out=ypst[k][:, hf * 512:(hf + 1) * 512], in_=pyp[hf][:].rearrange("p a b -> p (a b)")),
                        ["pyp%d" % hf], ["ypst%d" % k])
                stg(YP_d.ap()[j], ypst[k][:], "ypst%d" % k, wkey="YP_d")
            pieces = [(512, "q", 0), (1024, "q", 4), (1584, "x", 0)] + [(2096 + 512 * i, "m", 4 * i) for i in range(6)]
            pieces.append((1536, "g", 0))
            for pi, (col0, kind, b0) in enumerate(pieces):
                wbf = wbfs[(pi + 1) % 2]
                wk = "wbfb%d" % ((pi + 1) % 2)
                if pi + 1 < len(pieces):
                    nc0 = pieces[pi + 1][0]
                    ncols = 48 if pieces[pi + 1][1] == "g" else 512
                    load_w(wbfs[pi % 2], w_b[:, nc0:nc0 + ncols], ncols, "wbfb%d" % (pi % 2))
                if kind == "g":
                    break
                for blk in range(4):
                    for tb in range(NS // 4):
                        pq = pp[n % 2]
                        for c in range(8):
                            P.op("pe", lambda e, c=c, blk=blk, tb=tb, pq=pq, wbf=wbf: e.matmul(
                                pq[:], lhsT=wbf[:, c, blk * 128:(blk + 1) * 128],
                                rhs=hnT[:, c, tb * 512:(tb + 1) * 512], start=(c == 0), stop=(c == 7)),
                                [wk, "hnT_own"], ["ppb%d" % (n % 2)])
                        es = est[n % 2]
                        if kind == "q":
                            P.op("act", lambda e, es=es, pq=pq: e.mul(out=es[:], in_=pq[:], mul=0.125),
                                 ["ppb%d" % (n % 2)], ["estb%d" % (n % 2)])
                            dst = QT_d.ap()[b0 + blk, :, tb * 512:(tb + 1) * 512]
                        elif kind == "x":
                            P.op("act", lambda e, es=es, pq=pq: e.mul(out=es[:], in_=pq[:], mul=128.0 ** -0.5),
                                 ["ppb%d" % (n % 2)], ["estb%d" % (n % 2)])
                            dst = QX_d.ap()[b0 + blk, :, tb * 512:(tb + 1) * 512]
                        else:
                            P.op("act", lambda e, es=es, pq=pq: e.activation(out=es[:], in_=pq[:], func=AF.Sigmoid),
                                 ["ppb%d" % (n % 2)], ["estb%d" % (n % 2)])
                            dst = GM_d.ap()[b0 + blk, :, tb * 512:(tb + 1) * 512]
                        stg(dst, es[:], "estb%d" % (n % 2), wkey="scrB")
                        n += 1
            for tb in range(NS // 4):
                pq = pp[n % 2]
                for c in range(8):
                    P.op("pe", lambda e, c=c, tb=tb, pq=pq, wbf=wbf: e.matmul(
                        pq[0:48, :], lhsT=wbf[:, c, 0:48], rhs=hnT[:, c, tb * 512:(tb + 1) * 512],
                        start=(c == 0), stop=(c == 7)), [wk, "hnT_own"], ["ppb%d" % (n % 2)])
                P.op("act", lambda e, tb=tb, pq=pq: e.activation(out=GT[:, tb * 512:(tb + 1) * 512], in_=pq[0:48, :],
                                                                 func=AF.Sigmoid), ["ppb%d" % (n % 2)], ["GT"])
                n += 1
            P.barrier()
            P.emit()

        if stop_after == 'B':
            return nc
        with ExitStack() as sc:
            KT = sb(sc, "KT", [128, 2, T], BF16)
            KE = sb(sc, "KE", [128, 4, T], BF16)
            QN = [sb(sc, "QN%d" % i, [128, 4, 512], BF16) for i in range(2)]
            b31bc = sb(sc, "b31bc", [128, 16], F32)
            nselW = sb(sc, "nselW", [128, 128], F32)
            Vs = sb(sc, "Vs", [128, 32, 512], BF16)
            BT = sb(sc, "BT", [128, 9, 4, 512], BF16)
            ov32 = sb(sc, "ov32", [128, 128], F32)
            ovl = sb(sc, "ovl", [128, 2, 64], BF16)
            ones1 = sb(sc, "ones1", [128, 64], BF16)
            gsel = sb(sc, "gsel", [48, 48, 64], BF16)
            cm = sb(sc, "cm", [128, NS, 64], F32)
            addt = sb(sc, "addt", [128, NS, 64], F32)
            QT = [sb(sc, "QT%d" % i, [128, 8, 128], BF16) for i in range(2)]
            CB = [sb(sc, "CB%d" % i, [128, 2, 4, 512], BF16) for i in range(2)]
            PT = [sb(sc, "PT%d" % i, [128, 512], BF16) for i in range(4)]
            rz = sb(sc, "rz", [128, 512], F32)
            fac = sb(sc, "fac", [128, 512], F32)
            oaccs = [sb(sc, "oacc%d" % i, [64, 512], F32) for i in range(2)]
            otmp = sb(sc, "otmp", [128, 512], F32)
            impn = sb(sc, "impn", [128, 512], F32)
            impT = sb(sc, "impT", [64, 128], F32)
            impa = sb(sc, "impa", [128, 64], F32)
            impb = sb(sc, "impb", [128, 64], F32)
            v8a = sb(sc, "v8a", [128, 8], F32)
            v8b = sb(sc, "v8b", [128, 8], F32)
            onb = [sb(sc, "onb%d" % i, [64, 4, 512], BF16) for i in range(2)]
            pS = [ps(sc, "pS%d" % i, [128, 512], F32) for i in range(3)]
            pOZ = [ps(sc, "pOZ%d" % i, [128, 512], F32) for i in range(3)]
            pG = ps(sc, "pG", [128, 512], F32)
            pX = ps(sc, "pX", [128, 512], F32)

            for blk in range(2):
                ld(KT[:, blk, :], KT_d.ap()[2 + blk], "KT", reads=["KT_d"])
            for g in range(4):
                ld(KE[0:64, g, :], KT_d.ap()[g // 2, (g % 2) * 64:(g % 2) * 64 + 64, :], "KE", q="sp", reads=["KT_d"])
                ld(KE[64:128, g, :], c_expand, "KE", q="act")
            ld(b31bc[:], rel_bias[31:32, :].partition_broadcast(128), "b31bc")
            P.op("dve", lambda e: e.memset(nselW[:], 0.0), [], ["nselW"])
            for q_ in range(4):
                ld(Vs[:, q_ * 8:(q_ + 1) * 8, :], V_d.ap()[q_ * 8:(q_ + 1) * 8].rearrange("t p n -> p t n"), "Vs",
                   q=("sp", "act")[q_ % 2], reads=["V_d"])
            for ty in range(9):
                for g in range(4):
                    ld(BT[:, ty, g, :].rearrange("p (r q) -> p r q", r=4),
                       dap(fw_d, 4 * g * 2304 + ty * 256, [[1, 128], [2304, 4], [1, 128]]), "BT", q="act", reads=["fw_d"])
            ld(ov32[:], c_overlap, "ov32")
            P.op("dve", lambda e: e.tensor_copy(out=ovl[:].rearrange("p a b -> p (a b)"), in_=ov32[:]),
                 ["ov32"], ["ovl"])
            P.op("dve", lambda e: e.memset(ones1[:], 1.0), [], ["ones1"])
            ld(gsel[:].rearrange("p a b -> p (a b)"), c_gsel, "gsel")
            ld(cm[:].rearrange("p a b -> p (a b)"), c_cm, "cm")
            ld(addt[:].rearrange("p a b -> p (a b)"), c_addt, "addt")
            nS = 0
            nP = 0
            nA = 0
            tstate = [0]

            def t_chunk():
                n_ = tstate[0]
                if n_ >= 256 or not with_peer:
                    return
                tstate[0] += 1
                src, coff = ((peer_u, 0), (peer_v, D))[n_ // 128]
                it = n_ % 128
                i2 = n_ % 2
                P.dma("pool", lambda e: e.dma_start(out=UV16_d.ap()[it * 128:(it + 1) * 128, coff:coff + D],
                                                    in_=src[it * 128:(it + 1) * 128, :]), [], ["tbl16"])

            def c1_loads(j):
                k = j % 2
                ld(QT[k][:], QT_d.ap()[:, :, j * 128:(j + 1) * 128].rearrange("b p t -> p b t"), "QT%d" % k,
                   reads=["scrB"])
                for g in range(4):
                    ld(QN[k][0:64, g, :].rearrange("p (r q) -> p r q", r=4),
                       QT_d.ap()[(g // 2) * 4:(g // 2) * 4 + 4, (g % 2) * 64:(g % 2) * 64 + 64,
                                 j * 128:(j + 1) * 128].rearrange("b p t -> p b t"), "QNq%d_%d" % (k, g), reads=["scrB"])
                for ct in range(2):
                    for g in range(4):
                        off = 256 * j - 2048 * ct + 2048
                        ld(CB[k][:, ct, g, :].rearrange("p (r q) -> p r q", r=4),
                           dap(fc_d, 4 * g * 8192 + off, [[16, 128], [8192, 4], [1, 128]]), "CB%d" % k,
                           reads=["fc_d"])

            def c1_slot(j):
                nonlocal nS, nP, nA
                k = j % 2

                def issue_S(t):
                    nonlocal nS, nP
                    pq = pS[nS % 3]
                    pk = "pS%d" % (nS % 3)
                    nS += 1
                    extra = t["extra"]
                    lhsT_k = t["lk"]
                    rhs_ = t["rhs"]
                    if t["rhs3"]:
                        P.op("pe", lambda e: e.matmul(pq[:].rearrange("p (r q) -> p r q", r=4), lhsT=lhsT_k, rhs=rhs_,
                                                      start=True, stop=(len(extra) == 0)), t["kkeys"] + t["rkeys"], [pk])
                    else:
                        P.op("pe", lambda e: e.matmul(pq[:], lhsT=lhsT_k, rhs=rhs_,
                                                      start=True, stop=(len(extra) == 0)), t["kkeys"] + t["rkeys"], [pk])
                    for xi, (xl, xr, xk) in enumerate(extra):
                        P.op("pe", lambda e, xl=xl, xr=xr, xi=xi: e.matmul(
                            pq[:], lhsT=xl, rhs=xr, start=False, stop=(xi == len(extra) - 1)), xk, [pk])
                    pt = PT[nP % 4]
                    ptk = "PT%d" % (nP % 4)
                    nP += 1
                    P.op("act", lambda e: e.activation(out=pt[:], in_=pq[:], func=AF.Exp), [pk], [ptk])
                    t["pt"], t["ptk"] = pt, ptk

                def consume(t):
                    pt, ptk, acc, acck = t["pt"], t["ptk"], t["acc"], t["acck"]
                    first, last, vl = t["first"], t["last"], t["vl"]
                    P.op("pe", lambda e: e.matmul(acc[0:64, :], lhsT=vl, rhs=pt[:], start=first, stop=last),
                         t["vkeys"] + [ptk], [acck])
                    if t.get("z127"):
                        P.op("pe", lambda e: e.matmul(acc[64:128, :], lhsT=ones1[0:127, :], rhs=pt[0:127, :],
                                                      start=first, stop=last), ["ones1", ptk], [acck])
                    else:
                        P.op("pe", lambda e: e.matmul(acc[64:128, :], lhsT=ones1[:], rhs=pt[:], start=first, stop=last),
                             ["ones1", ptk], [acck])
                    if t["br"] == 0:
                        ct = t["ct"]
                        P.op("pe", lambda e: e.matmul(pG[64:128, :], lhsT=ovl[:, ct, :], rhs=pt[:],
                                                      start=first, stop=last), ["ovl", ptk], ["pI"])

                def finish(br, g, acc, acck):
                    oa = oaccs[g % 2]
                    oak = "oacc%d" % (g % 2)
                    P.op("dve", lambda e: e.tensor_scalar(out=rz[0:64, :], in0=acc[64:128, :], scalar1=1e-30, scalar2=None,
                                                          op0=ALU.max), [acck], ["rz"])
                    P.op("dve", lambda e: e.reciprocal(out=rz[0:64, :], in_=rz[0:64, :]), ["rz"], ["rz"])
                    if br == 0:
                        P.op("dve", lambda e: e.tensor_tensor(out=impn[0:64, :], in0=pG[64:128, :], in1=rz[0:64, :],
                                                              op=ALU.mult), ["pI", "rz"], ["impn"])
                        P.op("dve", lambda e: e.tensor_reduce(out=impT[:], in_=impn[0:64, :].rearrange("p (r q) -> p q r", r=4),
                                                              axis=AX.X, op=ALU.add), ["impn"], ["impT"])
                    for r in range(4):
                        P.op("pe", lambda e, r=r: e.matmul(
                            pG[0:64, r * 128:(r + 1) * 128], lhsT=gsel[:, br * 16 + 4 * g + r, :],
                            rhs=GT[:, j * 128:(j + 1) * 128], start=True, stop=True), ["gsel", "GT"], ["pG"])
                    P.op("dve", lambda e: e.tensor_tensor(out=fac[0:64, :], in0=pG[0:64, :], in1=rz[0:64, :], op=ALU.mult),
                         ["pG", "rz"], ["fac"])
                    if br == 0:
                        P.op("dve", lambda e: e.tensor_tensor(out=oa[0:64, :], in0=acc[0:64, :], in1=fac[0:64, :],
                                                              op=ALU.mult), [acck, "fac"], [oak])
                    else:
                        P.op("dve", lambda e: e.tensor_tensor(out=otmp[0:64, :], in0=acc[0:64, :], in1=fac[0:64, :],
                                                              op=ALU.mult), [acck, "fac"], ["otmp"])
                        if br == 1:
                            P.op("dve", lambda e: e.tensor_tensor(out=onb[k][0:64, g, :], in0=oa[0:64, :], in1=otmp[0:64, :],
                                                                  op=ALU.add), [oak, "otmp"], ["onb%d_%d" % (k, g)])
                        else:
                            P.op("dve", lambda e: e.tensor_tensor(out=oa[0:64, :], in0=oa[0:64, :], in1=otmp[0:64, :],
                                                                  op=ALU.add), [oak, "otmp"], [oak])
                    if br == 1:
                        hb = (g % 2) * 64
                        stg(ON_d.ap()[j][hb:hb + 64, (g // 2) * 512:(g // 2 + 1) * 512], onb[k][0:64, g, :],
                            "onb%d_%d" % (k, g), wkey="ON_d")

                def topk2(g):
                    P.op("pe", lambda e: e.transpose(out=pX[:, 0:64], in_=impT[:], identity=identf[0:64, 0:64]),
                         ["impT", "identf"], ["pX"])
                    P.op("dve", lambda e: e.tensor_tensor(out=impa[:], in0=pX[:, 0:64], in1=cm[:, j, :], op=ALU.mult),
                         ["pX", "cm"], ["impa"])
                    P.op("dve", lambda e: e.tensor_tensor(out=impa[:], in0=impa[:], in1=addt[:, j, :], op=ALU.add),
                         ["impa", "addt"], ["impa"])
                    P.op("dve", lambda e: e.max(out=v8a[:], in_=impa[:]), ["impa"], ["v8a"])
                    P.op("dve", lambda e: e.match_replace(out=impb[:], in_to_replace=v8a[:], in_values=impa[:],
                                                          imm_value=-3e38), ["impa", "v8a"], ["impb"])
                    P.op("dve", lambda e: e.max(out=v8b[:], in_=impb[:]), ["impb"], ["v8b"])
                    P.op("dve", lambda e: e.tensor_scalar(out=nselW[:, 64:128], in0=impa[:], scalar1=v8b[:, 7:8], scalar2=NEG,
                                                          op0=ALU.is_lt, op1=ALU.mult), ["impa", "v8b"], ["nselW"])

                def topk3(g):
                    P.op("pe", lambda e: e.transpose(out=pX[:, 128:256], in_=nselW[:], identity=identf[:]),
                         ["nselW", "identf"], ["pX"])
                    for r in range(4):
                        P.op("act", lambda e, r=r: e.activation(
                            out=QN[k][64:128, g, r * 128:(r + 1) * 128], in_=pX[64:128, 128:256], func=AF.Identity,
                            bias=b31bc[64:128, 4 * g + r:4 * g + r + 1], scale=1.0), ["pX", "b31bc"], ["QNn%d_%d" % (k, g)])

                def branch_tiles(br, g):
                    nonlocal nA
                    hb = (g % 2) * 64
                    hs = slice(hb, hb + 64)
                    pr = g // 2
                    qv = QT[k][hs, pr * 4:(pr + 1) * 4, :]
                    qkey = ["QT%d" % k]
                    acc, acck = pOZ[nA % 3], "pOZ%d" % (nA % 3)
                    nA += 1
                    out_ = []
                    if br == 0:
                        for ct in range(2):
                            out_.append(dict(br=0, g=g, ct=ct, lk=kcT[hs, pr, ct * 128:(ct + 1) * 128], kkeys=["kcT"],
                                             rhs=qv, rhs3=True, rkeys=qkey,
                                             extra=[(jb[:], CB[k][:, ct, g, :], ["jb", "CB%d" % k])],
                                             vl=vc[:, ct, g, :], vkeys=["vc"], z127=(ct == 1),
                                             first=(ct == 0), last=(ct == 1), acc=acc, acck=acck))
                    elif br == 2:
                        kts = [kt for kt in range(2 * j - 4, 2 * j + 2) if kt >= 0]
                        for ki, kt in enumerate(kts):
                            ty = 3 + kt - (2 * j - 4)
                            out_.append(dict(br=2, g=g, lk=KT[hs, pr, kt * 128:(kt + 1) * 128], kkeys=["KT"],
                                             rhs=qv, rhs3=True, rkeys=qkey,
                                             extra=[(jb[:], BT[:, ty, g, :], ["jb", "BT"])],
                                             vl=Vs[:, kt, 256 + g * 64:256 + (g + 1) * 64], vkeys=["Vs"],
                                             first=(ki == 0), last=(ki == len(kts) - 1), acc=acc, acck=acck))
                    else:
                        nk = 2 * j + 2
                        for kt in range(nk):
                            extra = []
                            if kt >= 2 * j - 1:
                                ty = kt - (2 * j - 1)
                                extra.append((jb[:], BT[:, ty, g, :], ["jb", "BT"]))
                            out_.append(dict(br=1, g=g, lk=KE[:, g, kt * 128:(kt + 1) * 128], kkeys=["KE"], extra=extra,
                                             rhs=QN[k][:, g, :], rhs3=False,
                                             rkeys=["QNq%d_%d" % (k, g), "QNn%d_%d" % (k, g)],
                                             vl=Vs[:, kt, g * 64:(g + 1) * 64], vkeys=["Vs"],
                                             first=(kt == 0), last=(kt == nk - 1), acc=acc, acck=acck))
                    return out_

                order = [(0, 0), (2, 0), (0, 1), (2, 1), (1, 0), (0, 2), (2, 2), (1, 1), (0, 3), (2, 3), (1, 2), (1, 3)]
                seq = []
                for (br, g) in order:
                    seq.extend(branch_tiles(br, g))
                for pos_, t in enumerate(seq):
                    t["F"] = (issue_S, consume, finish, topk2, topk3)
                    t["j"] = j
                    t["pos"] = pos_
                    t["nseq"] = len(seq)
                return seq

            c1_loads(0)
            allseq = []
            for j in range(NS):
                allseq.extend(c1_slot(j))
            LA = 2
            pend = []
            for idx in range(len(allseq) + LA):
                if idx < len(allseq):
                    t = allseq[idx]
                    if t["pos"] == 0 and t["j"] + 1 < NS:
                        c1_loads(t["j"] + 1)
                    if t["br"] == 1 and t["first"]:
                        keep = []
                        for p_ in pend:
                            if p_[2] == t["g"] and p_[3] == t["j"]:
                                p_[1](p_[2])
                            else:
                                keep.append(p_)
                        pend = keep
                    t["F"][0](t)
                    if (t["pos"] * 16) // t["nseq"] != ((t["pos"] + 1) * 16) // t["nseq"]:
                        t_chunk()
                keep = []
                for p_ in pend:
                    if p_[0] <= idx:
                        p_[1](p_[2])
                    else:
                        keep.append(p_)
                pend = keep
                if idx >= LA:
                    t = allseq[idx - LA]
                    t["F"][1](t)
                    if t["last"]:
                        t["F"][2](t["br"], t["g"], t["acc"], t["acck"])
                        if t["br"] == 0:
                            pend.append((idx + 2, t["F"][3], t["g"], t["j"]))
                            pend.append((idx + 4, t["F"][4], t["g"], t["j"]))
            for p_ in pend:
                p_[1](p_[2])
            for _ in range(256):
                t_chunk()
            P.barrier()
            P.emit()
        if stop_after == 'C1':
            return nc
        with ExitStack() as sd:
            Wno = sb(sd, "Wno", [128, 8, D], BF16)
            Wxo = sb(sd, "Wxo", [128, 4, D], BF16)
            Wo = sb(sd, "Wo", [128, 8, D], BF16)
            Wm = sb(sd, "Wm", [128, 8, D], BF16)
            gtm = sb(sd, "gt_m", [128, D], F32)
            hnTm = sb(sd, "hnTm", [128, 8, 256], BF16)
            KmT = sb(sd, "KmT", [128, 4, 256], BF16)
            Vm = sb(sd, "Vm", [128, 2, 512], BF16)
            bufs = norm_bufs(sd)
            onT = [sb(sd, "onTc%d" % i, [128, 8, 128], BF16) for i in range(2)]
            qxT = [sb(sd, "qxT%d" % i, [128, 4, 128], BF16) for i in range(2)]
            gm = [sb(sd, "gm%d" % i, [128, 24, 128], BF16) for i in range(2)]
            yp = [sb(sd, "yp%d" % i, [128, 8, 128], BF16) for i in range(2)]
            xo = [sb(sd, "xo%d" % i, [128, D], F32) for i in range(2)]
            acc = sb(sd, "acc", [128, 8, 128], F32)
            tmp = sb(sd, "tmpc", [128, 8, 128], F32)
            tmp2 = sb(sd, "tmp2c", [128, 8, 128], F32)
            mrgT = sb(sd, "mrgT", [128, 8, 128], BF16)
            x1t = [sb(sd, "x1t%d" % i, [128, D], F32) for i in range(2)]
            PTm = [sb(sd, "PTm%d" % i, [128, 512], BF16) for i in range(2)]
            oxT = sb(sd, "oxT", [128, 4, 128], BF16)
            rzm = sb(sd, "rzm", [128, 512], F32)
            pY = [ps(sd, "pY%d" % i, [128, 4, 128], F32) for i in range(2)]
            pS2 = [ps(sd, "pSm%d" % i, [128, 512], F32) for i in range(2)]
            pO2 = ps(sd, "pOm", [128, 512], F32)
            pZ2 = ps(sd, "pZm", [128, 512], F32)

            def ldc(dst, src, key):
                P.dma("pool", lambda e: e.dma_start(out=dst, in_=src), [], [key])
            for half in range(2):
                for a_ in range(2):
                    ldc(Wno[half * 64:(half + 1) * 64, a_ * 4:(a_ + 1) * 4, :],
                        w_nsa_out.rearrange("(a hf r d) n -> hf a d r n", a=2, hf=2, r=4, d=64)[half, a_], "Wno")
            for q_ in range(2):
                ldc(Wm[:, q_ * 4:(q_ + 1) * 4, :], w_mem_kv.rearrange("(c p) n -> p c n", p=128)[:, q_ * 4:(q_ + 1) * 4, :], "Wm")
            for q_ in range(2):
                ldc(Wo[:, q_ * 4:(q_ + 1) * 4, :], w_o.rearrange("(c p) n -> p c n", p=128)[:, q_ * 4:(q_ + 1) * 4, :], "Wo")
            ldc(Wxo[:], w_x_out.rearrange("(c p) n -> p c n", p=128), "Wxo")
            ld(gtm[:], g_mem.partition_broadcast(128), "gt_m")
            for i in range(2):
                norm_tile(bufs, mem[i * 128:(i + 1) * 128, :], "gt_m", gtm, hnTm[:, :, i * 128:(i + 1) * 128], "hnTm", i)
            for hx in range(4):
                for c in range(8):
                    P.op("pe", lambda e, c=c, hx=hx: e.matmul(pS2[hx % 2][:, 0:256], lhsT=Wm[:, c, hx * 128:(hx + 1) * 128],
                                                              rhs=hnTm[:, c, :], start=(c == 0), stop=(c == 7)),
                         ["Wm", "hnTm"], ["pSm%d" % (hx % 2)])
                P.op("act", lambda e, hx=hx: e.copy(out=KmT[:, hx, :], in_=pS2[hx % 2][:, 0:256]),
                     ["pSm%d" % (hx % 2)], ["KmT"])
            for mt in range(2):
                for c in range(8):
                    P.op("pe", lambda e, c=c, mt=mt: e.matmul(pS2[mt][:], lhsT=hnTm[:, c, mt * 128:(mt + 1) * 128],
                                                              rhs=Wm[:, c, 512:1024], start=(c == 0), stop=(c == 7)),
                         ["Wm", "hnTm"], ["pSm%d" % mt])
                P.op("act", lambda e, mt=mt: e.copy(out=Vm[:, mt, :], in_=pS2[mt][:]), ["pSm%d" % mt], ["Vm"])

            if stop_after == 'C2a':
                P.barrier()
                P.emit()
                return nc

            def c2_slot(j):
                k = j % 2
                ld(onT[k][:], ON_d.ap()[j].rearrange("p (a t) -> p a t", a=8), "onTc%d" % k, reads=["ON_d"])
                ld(qxT[k][:], QX_d.ap()[:, :, j * 128:(j + 1) * 128].rearrange("b p t -> p b t"), "qxT%d" % k,
                   reads=["scrB"])
                ld(gm[k][:], GM_d.ap()[:, :, j * 128:(j + 1) * 128].rearrange("b p t -> p b t"), "gm%d" % k,
                   reads=["scrB"])
                ld(yp[k][:], YP_d.ap()[j].rearrange("p (a t) -> p a t", a=8), "yp%d" % k, reads=["YP_d"])
                ld(xo[k][:], x_own[j * 128:(j + 1) * 128, :], "xo%d" % k)
                if C2STOP <= 1:
                    return
                for hf in range(2):
                    for m4 in range(4):
                        m = hf * 4 + m4
                        for hh in range(8):
                            P.op("pe", lambda e, m=m, m4=m4, hf=hf, hh=hh: e.matmul(
                                pY[hf][:, m4, :], lhsT=Wno[:, hh, m * 128:(m + 1) * 128],
                                rhs=onT[k][:, hh, :], start=(hh == 0), stop=(hh == 7)),
                                ["Wno", "onTc%d" % k], ["pY%d" % hf])
                    P.op("dve", lambda e, hf=hf: e.tensor_tensor(out=acc[:, hf * 4:(hf + 1) * 4, :], in0=pY[hf][:],
                                                                 in1=gm[k][:, 8 + hf * 4:8 + (hf + 1) * 4, :], op=ALU.mult),
                         ["pY%d" % hf, "gm%d" % k], ["acc"])
                if C2STOP <= 2:
                    return
                for mt in range(2):
                    for hx in range(4):
                        P.op("pe", lambda e, hx=hx, mt=mt: e.matmul(
                            pS2[mt][:, hx * 128:(hx + 1) * 128], lhsT=KmT[:, hx, mt * 128:(mt + 1) * 128],
                            rhs=qxT[k][:, hx, :], start=True, stop=True), ["KmT", "qxT%d" % k], ["pSm%d" % mt])
                    P.op("act", lambda e, mt=mt: e.activation(out=PTm[mt][:], in_=pS2[mt][:], func=AF.Exp),
                         ["pSm%d" % mt], ["PTm%d" % mt])
                if C2STOP <= 3:
                    return
                for mt in range(2):
                    P.op("pe", lambda e, mt=mt: e.matmul(pZ2[:], lhsT=onesb[:], rhs=PTm[mt][:], start=(mt == 0),
                                                         stop=(mt == 1)), ["onesb", "PTm%d" % mt], ["pZm"])
                for hx in range(4):
                    for mt in range(2):
                        P.op("pe", lambda e, hx=hx, mt=mt: e.matmul(
                            pO2[:, hx * 128:(hx + 1) * 128], lhsT=Vm[:, mt, hx * 128:(hx + 1) * 128],
                            rhs=PTm[mt][:, hx * 128:(hx + 1) * 128], start=(mt == 0), stop=(mt == 1)),
                            ["Vm", "PTm%d" % mt], ["pOm"])
                P.op("dve", lambda e: e.reciprocal(out=rzm[:], in_=pZ2[:]), ["pZm"], ["rzm"])
                P.op("dve", lambda e: e.tensor_tensor(out=oxT[:].rearrange("p a b -> p (a b)"), in0=pO2[:], in1=rzm[:],
                                                      op=ALU.mult), ["pOm", "rzm"], ["oxT"])
                if C2STOP <= 4:
                    return
                for hf in range(2):
                    for m4 in range(4):
                        m = hf * 4 + m4
                        for hx in range(4):
                            P.op("pe", lambda e, hx=hx, m=m, m4=m4, hf=hf: e.matmul(
                                pY[hf][:, m4, :], lhsT=Wxo[:, hx, m * 128:(m + 1) * 128], rhs=oxT[:, hx, :],
                                start=(hx == 0), stop=(hx == 3)), ["Wxo", "oxT"], ["pY%d" % hf])
                    P.op("dve", lambda e, hf=hf: e.tensor_tensor(out=tmp[:, hf * 4:(hf + 1) * 4, :], in0=pY[hf][:],
                                                                 in1=gm[k][:, 16 + hf * 4:16 + (hf + 1) * 4, :], op=ALU.mult),
                         ["pY%d" % hf, "gm%d" % k], ["tmpc"])
                if C2STOP <= 5:
                    return
                P.op("pool", lambda e: e.tensor_tensor(out=tmp2[:], in0=gm[k][:, 0:8, :], in1=yp[k][:], op=ALU.mult),
                     ["gm%d" % k, "yp%d" % k], ["tmp2c"])
                P.op("pool", lambda e: e.tensor_tensor(out=tmp2[:], in0=tmp2[:], in1=tmp[:], op=ALU.add),
                     ["tmp2c", "tmpc"], ["tmp2c"])
                P.op("dve", lambda e: e.tensor_tensor(out=mrgT[:], in0=acc[:], in1=tmp2[:], op=ALU.add),
                     ["acc", "tmp2c"], ["mrgT"])
                if C2STOP <= 6:
                    return
                for nh in range(2):
                    for c in range(8):
                        P.op("pe", lambda e, c=c, nh=nh: e.matmul(
                            pY[nh][:].rearrange("p a b -> p (a b)"), lhsT=mrgT[:, c, :], rhs=Wo[:, c, nh * 512:(nh + 1) * 512],
                            start=(c == 0), stop=(c == 7)), ["mrgT", "Wo"], ["pY%d" % nh])
                    P.op("dve", lambda e, nh=nh: e.tensor_tensor(
                        out=x1t[k][:, nh * 512:(nh + 1) * 512], in0=pY[nh][:].rearrange("p a b -> p (a b)"),
                        in1=xo[k][:, nh * 512:(nh + 1) * 512], op=ALU.add), ["pY%d" % nh, "xo%d" % k], ["x1t%d" % k])
                stg(X1_d.ap()[j * 128:(j + 1) * 128, :], x1t[k][:], "x1t%d" % k, wkey="X1_d")
            for j in range(1 if stop_after == 'C2b' else NS):
                c2_slot(j)
            P.barrier()
            P.emit()

        if stop_after == 'C2':
            return nc
        if with_peer:
          with ExitStack() as se:
            Wq = sb(se, "Wq", [128, 8, 2048], BF16)
            skT = sb(se, "skT", [128, 16, 128], F32)
            gff = sb(se, "gff", [128, D], F32)
            gfn = sb(se, "gfn", [128, D], F32)
            bufs = norm_bufs(se, with_xt=False)
            hn32b = [sb(se, "hn32_%d" % i, [128, D], F32) for i in range(2)]
            x1b = [sb(se, "x1b%d" % i, [128, D], F32) for i in range(3)]
            hn2T = sb(se, "hn2T", [128, 8, 128], BF16)
            qT = sb(se, "qT", [128, 16, 128], F32)
            S = sb(se, "S", [128, 16, 128], F32)
            S2 = sb(se, "S2", [128, 128], F32)
            tv = sb(se, "tv", [128, 16, 16], F32)
            ti = sb(se, "ti", [128, 16, 16], U32)
            tif = sb(se, "tif", [128, 16, 16], F32)
            ti0 = sb(se, "ti0", [128, 8, 16], F32)
            cs_ = sb(se, "cands", [128, 8, 256], F32)
            cid = sb(se, "candid", [128, 8, 256], F32)
            c2 = sb(se, "c2", [128, 256], F32)
            junk = sb(se, "junk", [128, D], F32)
            bs = sb(se, "bs", [128, 8, 16], F32)
            bpos = sb(se, "bpos", [128, 8, 16], U32)
            pu = sb(se, "pu", [128, 2, 8, 16], U32)
            pf = sb(se, "pf", [128, 2, 8, 16], F32)
            idp = sb(se, "idp", [128, 2, 128], F32)
            iota = sb(se, "iota", [128, 256], F32)
            nb = sb(se, "nb", [128, 8], F32)
            zs_ = sb(se, "zs", [128, 8], F32)
            idf = sb(se, "idf", [128, 128], F32)
            ids2 = [sb(se, "ids%d" % i, [128, 128], U32) for i in range(3)]
            eg = sb(se, "eg", [128, 8, 16], F32)
            gateb = [sb(se, "gate%d" % i, [128, 8, 16], F32) for i in range(2)]
            actv = sb(se, "actv", [128, 128], F32)
            NB = 12
            UVb = [sb(se, "UVb%d" % i, [128, 2 * D], BF16) for i in range(NB)]
            dg = [sb(se, "dg%d" % i, [128, 128], BF16) for i in range(4)]
            dgA = [sb(se, "dgA%d" % i, [128, 128], BF16) for i in range(4)]
            gsb = sb(se, "gsb", [128, 8], F32)
            x2 = sb(se, "x2", [128, D], F32)
            ss2 = sb(se, "ss2", [128, 1], F32)
            rs2 = sb(se, "rs2", [128, 1], F32)
            outt = sb(se, "outt", [128, D], F32)
            pQ = [ps(se, "pQ%d" % i, [128, 4, 128], F32) for i in range(2)]
            pS3 = [ps(se, "pS3%d" % i, [128, 4, 128], F32) for i in range(2)]
            pV = [ps(se, "pV%d" % i, [128, 512], F32) for i in range(2)]

            for i in range(4):
                P.dma("pool", lambda e, i=i: e.dma_start(
                    out=Wq[:, :, i * 512:(i + 1) * 512],
                    in_=w_peer_q[:, i * 512:(i + 1) * 512].rearrange("(c p) n -> p c n", p=128)), [], ["Wq"])
            sk32 = cs_[:].rearrange("p a (b c) -> p (a b) c", c=128)
            ld(sk32, sub_keys.rearrange("a k c -> k a c"), "cands")
            for hp in range(16):
                P.op("pe", lambda e, hp=hp: e.transpose(out=pQ[(hp // 4) % 2][:, hp % 4, :], in_=sk32[:, hp, :],
                                                        identity=identf[:]), ["cands", "identf"], ["pQ%d" % ((hp // 4) % 2)])
                if hp % 4 == 3:
                    P.op("act", lambda e, hp=hp: e.copy(out=skT[:, hp - 3:hp + 1, :], in_=pQ[(hp // 4) % 2][:]),
                         ["pQ%d" % ((hp // 4) % 2)], ["skT"])
            ld(gff[:], g_ffn.partition_broadcast(128), "gff")
            ld(iota[:], c_iota, "iota")
            ld(gfn[:], g_final.partition_broadcast(128), "gfn")

            def d_front(j):
                xt1 = x1b[j % 3]
                xk = "x1b%d" % (j % 3)
                ids = ids2[j % 3]
                idk = "ids%d" % (j % 3)
                hn32 = hn32b[j % 2]
                hnk = "hn32_%d" % (j % 2)
                gate = gateb[j % 2]
                gk = "gate%d" % (j % 2)
                P.dma("sp", lambda e: e.dma_start(out=xt1[:], in_=X1_d.ap()[j * 128:(j + 1) * 128, :]),
                      ["X1_d"], [xk])
                norm_tile(bufs, None, "gff", gff, hn2T[:], "hn2T", j, hn32=(hn32[:], hnk), skip_load=True,
                          xt_ovr=(xt1, xk))
                yield
                for hp in range(16):
                    b = (hp // 4) % 2
                    for c in range(8):
                        P.op("pe", lambda e, hp=hp, c=c, b=b: e.matmul(
                            pQ[b][:, hp % 4, :], lhsT=Wq[:, c, hp * 128:(hp + 1) * 128], rhs=hn2T[:, c, :],
                            start=(c == 0), stop=(c == 7)), ["Wq", "hn2T"], ["pQ%d" % b])
                    if hp % 4 == 3:
                        P.op("act", lambda e, hp=hp, b=b: e.copy(out=qT[:, hp - 3:hp + 1, :], in_=pQ[b][:]),
                             ["pQ%d" % b], ["qT"])
                    yield
                for hp in range(16):
                    b = (hp // 4) % 2
                    P.op("pe", lambda e, hp=hp, b=b: e.matmul(pS3[b][:, hp % 4, :], lhsT=qT[:, hp, :], rhs=skT[:, hp, :],
                                                              start=True, stop=True), ["qT", "skT"], ["pS3%d" % b])
                    if hp % 4 == 3:
                        P.op("act", lambda e, hp=hp, b=b: e.copy(out=S[:, hp - 3:hp + 1, :], in_=pS3[b][:]),
                             ["pS3%d" % b], ["S"])
                    yield
                for hp in range(16):
                    P.op("dve", lambda e, hp=hp: e.max(out=tv[:, hp, 0:8], in_=S[:, hp, :]), ["S"], ["tv"])
                    P.op("dve", lambda e, hp=hp: e.max_index(out=ti[:, hp, 0:8], in_max=tv[:, hp, 0:8], in_values=S[:, hp, :]),
                         ["S", "tv"], ["ti"])
                    yield
                    P.op("dve", lambda e, hp=hp: e.match_replace(out=S2[:], in_to_replace=tv[:, hp, 0:8],
                                                                 in_values=S[:, hp, :], imm_value=-3e38), ["S", "tv"], ["S2"])
                    P.op("dve", lambda e, hp=hp: e.max(out=tv[:, hp, 8:16], in_=S2[:]), ["S2"], ["tv"])
                    P.op("dve", lambda e, hp=hp: e.max_index(out=ti[:, hp, 8:16], in_max=tv[:, hp, 8:16], in_values=S2[:]),
                         ["S2", "tv"], ["ti"])
                    yield
                P.op("dve", lambda e: e.tensor_copy(out=tif[:], in_=ti[:]), ["ti"], ["tif"])
                tv4 = tv[:].rearrange("p (h two) k -> p h two k", two=2)
                tf4 = tif[:].rearrange("p (h two) k -> p h two k", two=2)
                P.op("dve", lambda e: e.tensor_scalar(out=ti0[:], in0=tf4[:, :, 0, :], scalar1=128.0, scalar2=None,
                                                      op0=ALU.mult), ["tif"], ["ti0"])
                yield
                P.op("dve", lambda e: e.tensor_tensor(
                    out=cs_[:].rearrange("p h (i j) -> p h i j", i=16),
                    in0=tv4[:, :, 0, :].unsqueeze(3).to_broadcast([128, 8, 16, 16]),
                    in1=tv4[:, :, 1, :].unsqueeze(2).to_broadcast([128, 8, 16, 16]), op=ALU.add), ["tv"], ["cands"])
                yield
                for hh in range(8):
                    P.op("dve", lambda e, hh=hh: e.max(out=bs[:, hh, 0:8], in_=cs_[:, hh, :]), ["cands"], ["bs"])
                    P.op("dve", lambda e, hh=hh: e.max_index(out=bpos[:, hh, 0:8], in_max=bs[:, hh, 0:8],
                                                             in_values=cs_[:, hh, :]), ["cands", "bs"], ["bpos"])
                    yield
                    P.op("dve", lambda e, hh=hh: e.match_replace(out=c2[:], in_to_replace=bs[:, hh, 0:8],
                                                                 in_values=cs_[:, hh, :], imm_value=-3e38),
                         ["cands", "bs"], ["c2"])
                    P.op("dve", lambda e, hh=hh: e.max(out=bs[:, hh, 8:16], in_=c2[:]), ["c2"], ["bs"])
                    yield
                    P.op("dve", lambda e, hh=hh: e.max_index(out=bpos[:, hh, 8:16], in_max=bs[:, hh, 8:16],
                                                             in_values=c2[:]), ["c2", "bs"], ["bpos"])
                    yield
                P.op("dve", lambda e: e.tensor_single_scalar(out=pu[:, 0], in_=bpos[:], scalar=4, op=ALU.logical_shift_right),
                     ["bpos"], ["pu"])
                P.op("dve", lambda e: e.tensor_single_scalar(out=pu[:, 1], in_=bpos[:], scalar=15, op=ALU.bitwise_and),
                     ["bpos"], ["pu"])
                P.op("dve", lambda e: e.tensor_copy(out=pf[:], in_=pu[:]), ["pu"], ["pf"])
                yield
                oh4 = cid[:].rearrange("p h (k c) -> p h k c", c=16)
                oh3 = cid[:].rearrange("p h (k c) -> p (h k) c", c=16)
                io4 = iota[:, 0:16].unsqueeze(1).unsqueeze(1).to_broadcast([128, 8, 16, 16])
                for w_ in range(2):
                    tbl = ti0[:] if w_ == 0 else tf4[:, :, 1, :]
                    tkey = "ti0" if w_ == 0 else "tif"
                    P.op("dve", lambda e, w_=w_: e.tensor_tensor(
                        out=oh4, in0=io4, in1=pf[:, w_].unsqueeze(3).to_broadcast([128, 8, 16, 16]), op=ALU.is_equal),
                        ["iota", "pf"], ["candid"])
                    yield
                    P.op("dve", lambda e, tbl=tbl: e.tensor_tensor(
                        out=oh4, in0=oh4, in1=tbl.unsqueeze(2).to_broadcast([128, 8, 16, 16]), op=ALU.mult),
                        ["candid", tkey], ["candid"])
                    yield
                    P.op("dve", lambda e, w_=w_: e.tensor_reduce(out=idp[:, w_, :], in_=oh3, axis=AX.X, op=ALU.add),
                         ["candid"], ["idp"])
                    yield
                P.op("dve", lambda e: e.tensor_tensor(out=idf[:], in0=idp[:, 0, :], in1=idp[:, 1, :], op=ALU.add),
                     ["idp"], ["idf"])
                P.op("dve", lambda e: e.tensor_copy(out=ids[:], in_=idf[:]), ["idf"], [idk])
                P.op("dve", lambda e: e.tensor_scalar(out=nb[:], in0=bs[:, :, 0], scalar1=-1.0, scalar2=None, op0=ALU.mult),
                     ["bs"], ["nb"])
                yield
                for hh in range(8):
                    P.op("act", lambda e, hh=hh: e.activation(out=eg[:, hh, :], in_=bs[:, hh, :], func=AF.Exp,
                                                              bias=nb[:, hh:hh + 1], scale=1.0, accum_out=zs_[:, hh:hh + 1]),
                         ["bs", "nb"], ["eg", "zs"])
                P.op("dve", lambda e: e.reciprocal(out=zs_[:], in_=zs_[:]), ["zs"], ["zs"])
                P.op("dve", lambda e: e.tensor_tensor(out=gate[:], in0=eg[:], in1=zs_[:].unsqueeze(2).to_broadcast([128, 8, 16]),
                                                      op=ALU.mult), ["eg", "zs"], [gk])
                yield

            uev = {}
            vev = {}

            def g_step(j, s_):
                ids = ids2[j % 3]
                idk = "ids%d" % (j % 3)
                hn32 = hn32b[j % 2]
                hnk = "hn32_%d" % (j % 2)
                gate = gateb[j % 2]
                gk = "gate%d" % (j % 2)
                b = s_ % NB
                uvb = UVb[b]
                uk = "UVb%d" % b
                d_ = dg[s_ % 4]
                dk = "dg%d" % (s_ % 4)
                dA = dgA[s_ % 4]
                dAk = "dgA%d" % (s_ % 4)
                ak = "actv%d" % (s_ % 8)
                gsk = "gsb%d" % (s_ % 8)
                if s_ >= NB and s_ % 4 == 0:
                    P._wait("pool", uev[(j, s_ - (NB - 3))])
                    P._wait("pool", vev[(j, s_ - (NB - 3))])
                P.dma("pool", lambda e: e.indirect_dma_start(
                    out=uvb[:], out_offset=None, in_=UV16_d.ap(),
                    in_offset=bass.IndirectOffsetOnAxis(ap=ids[:, s_:s_ + 1], axis=0)), [idk, "tbl16"], [uk],
                    own="G%d" % b)
                uev[(j, s_)] = P.op("dve", lambda e: e.scalar_tensor_tensor(
                    out=junk[:], in0=uvb[:, 0:D], scalar=1.0, in1=hn32[:], op0=ALU.mult, op1=ALU.mult,
                    accum_out=actv[:, s_:s_ + 1]), [uk, hnk], ["junk", ak])
                P.op("act", lambda e: e.activation(out=gsb[:, s_ % 8:s_ % 8 + 1], in_=actv[:, s_:s_ + 1], func=AF.Gelu),
                     [ak], [gsk])
                P.op("act", lambda e: e.activation(out=dA[:], in_=identb[:], func=AF.Copy,
                                                   scale=gsb[:, s_ % 8:s_ % 8 + 1]), ["identb", gsk], [dAk])
                P.op("act", lambda e: e.activation(out=d_[:], in_=dA[:], func=AF.Copy,
                                                   scale=gate[:].rearrange("p a b -> p (a b)")[:, s_:s_ + 1]), [dAk, gk], [dk])
                for nh in range(2):
                    vev[(j, s_)] = P.op("pe", lambda e, nh=nh: e.matmul(
                        pV[nh][:], lhsT=d_[:], rhs=uvb[:, D + nh * 512:D + (nh + 1) * 512], start=(s_ == 0), stop=(s_ == 127)),
                        [dk, uk], ["pV%d" % nh])

            def v_end(j):
                xt1 = x1b[j % 3]
                xk = "x1b%d" % (j % 3)
                for nh in range(2):
                    P.op("dve", lambda e, nh=nh: e.tensor_tensor(out=x2[:, nh * 512:(nh + 1) * 512], in0=pV[nh][:],
                                                                 in1=xt1[:, nh * 512:(nh + 1) * 512], op=ALU.add),
                         ["pV%d" % nh, xk], ["x2"])
                P.op("act", lambda e: e.activation(out=junk[:], in_=x2[:], func=AF.Square, accum_out=ss2[:]),
                     ["x2"], ["junk", "ss2"])
                P.op("act", lambda e: e.activation(out=rs2[:], in_=ss2[:], func=AF.Sqrt, scale=1.0 / D, bias=1e-6),
                     ["ss2"], ["rs2"])
                P.op("dve", lambda e: e.reciprocal(out=rs2[:], in_=rs2[:]), ["rs2"], ["rs2"])
                P.op("dve", lambda e: e.scalar_tensor_tensor(out=outt[:], in0=x2[:], scalar=rs2[:], in1=gfn[:],
                                                              op0=ALU.mult, op1=ALU.mult), ["x2", "rs2", "gfn"], ["outt"])
                stg(out[j * 128:(j + 1) * 128, :], outt[:], "outt", wkey="out")

            for _ in d_front(0):
                pass
            for j in range(NS):
                fr = d_front(j + 1) if j + 1 < NS else None
                for s_ in range(128):
                    g_step(j, s_)
                    if fr is not None:
                        for _ in range(2 if s_ % 3 == 0 else 1):
                            if next(fr, "done") == "done":
                                fr = None
                                break
                if fr is not None:
                    for _ in fr:
                        pass
                v_end(j)
            P.barrier()
            P.emit()
    return nc


def _make_in_maps(inputs):
    f = lambda k: np.ascontiguousarray(np.asarray(inputs[k], dtype=np.float32))
    x = f("x")
    mem = f("mem")
    w_in = f("w_in")[0]
    cols_a = np.concatenate([np.arange(1536, 1792), np.arange(1792, 2048), np.arange(2048, 2304),
                             np.arange(2560, 2816), np.arange(2304, 2560), np.arange(2816, 3072)])
    qcols = []
    for p in range(2):
        for r in range(4):
            for hh in (8 * p + r, 8 * p + 4 + r):
                qcols.append(512 + hh * 64 + np.arange(64))
    cols_b = np.concatenate([np.arange(0, 512)] + qcols + [np.arange(3072, 3120), np.arange(3120, 3632),
                                                          np.arange(3632, 6704)])
    w_a = np.ascontiguousarray(w_in[:, cols_a])
    w_b = np.ascontiguousarray(w_in[:, cols_b])
    shared = {
        "w_a": w_a, "w_b": w_b, "rel_bias": f("rel_bias"), "g_mix": f("g_mix").reshape(1, D),
        "w_pool_grp": f("w_pool_grp")[0], "pool_scale": f("pool_scale").reshape(4, 128),
        "w_pool_out": f("w_pool_out")[0], "w_cmp_k": f("w_cmp_k")[0], "w_cmp_v": f("w_cmp_v")[0],
        "pe_k": f("pe_k")[0], "pe_v": f("pe_v")[0], "w_nsa_out": f("w_nsa_out")[0],
        "g_mem": f("g_mem").reshape(1, D), "w_mem_kv": f("w_mem_kv")[0], "w_x_out": f("w_x_out")[0],
        "w_o": f("w_o")[0], "g_ffn": f("g_ffn").reshape(1, D), "w_peer_q": f("w_peer_q")[0],
        "sub_keys": f("peer_sub_keys")[0].reshape(16, 128, 128), "peer_u": f("peer_u")[0],
        "peer_v": f("peer_v")[0], "g_final": f("g_final").reshape(1, D),
    }
    consts = [_consts(0), _consts(1)]
    in_maps = []
    for c in range(8):
        b, h = c // 2, c % 2
        xt = x[b].reshape(32, 128, D)
        own = xt[h::2]
        if h == 1:
            prev = xt[0::2]
        else:
            prev = np.concatenate([np.zeros((1, 128, D), np.float32), xt[1::2][:15]], 0)
        m = dict(shared)
        m["x_all"] = np.ascontiguousarray(x[b])
        m["x_own"] = np.ascontiguousarray(own.reshape(NS * 128, D))
        m["x_prev"] = np.ascontiguousarray(prev.reshape(NS * 128, D))
        m["mem"] = np.ascontiguousarray(mem[b])
        for k, v in consts[h].items():
            m["c_" + k] = v
        in_maps.append(m)
    return in_maps


def _assemble(res, key="out"):
    outf = np.zeros((4, 32, 128, D), np.float32)
    for c in range(8):
        b, h = c // 2, c % 2
        outf[b, h::2] = np.asarray(res.results[c][key]).reshape(NS, 128, D)
    return outf.reshape(4, T, D)


def kernel(**inputs):
    in_maps = _make_in_maps(inputs)
    nc = build_program()
    res = run_bass_kernel_spmd(nc, in_maps, core_ids=list(range(8)))
    return _assemble(res)
```
